# Optimizing a Trainium2 kernel written in Bass

```python
import jax, jax.numpy as jnp
from jax import lax
import numpy as np

D_MODEL = 2048
BATCH = 16
SEQ = 2048
DEPTH = 4

HEAD_DIM = 64
ROPE_THETA = 10000.0
EPS = 1e-6
BLOCK_Q = 128
NEG = -1e30
D_FF = 4 * D_MODEL
A_HEADS = D_MODEL // 256
A_WIDTH = A_HEADS * HEAD_DIM
A_Q_RANK = 3 * D_MODEL // 16
IDX_HEADS = 16
IDX_DIM = HEAD_DIM
TOPK_MAX = 256
B_HEADS = 3 * D_MODEL // 512
B_WIDTH = B_HEADS * HEAD_DIM
DILATED_PATTERNS = ((128, 1), (512, 4), (2048, 16))
POOL_WINDOWS = (2, 4, 8, 16)
C_WIDTH = 3 * D_MODEL // 8
POOL_GROUPS = len(POOL_WINDOWS)
POOL_GROUP_DIM = C_WIDTH // POOL_GROUPS
IN_SIZES = (A_Q_RANK, HEAD_DIM, HEAD_DIM, IDX_DIM, IDX_HEADS, B_WIDTH, B_WIDTH, B_WIDTH, C_WIDTH)
IN_COLS = sum(IN_SIZES)
IN_SPLITS = tuple(int(v) for v in np.cumsum(IN_SIZES)[:-1])
MIX_WIDTH = A_WIDTH + B_WIDTH + C_WIDTH

kernel_name = "hybrid_dsa_dilated_pool_trunk"


def rms_norm(x, g):
    x32 = x.astype(jnp.float32)
    y = x32 * lax.rsqrt(jnp.mean(x32 * x32, axis=-1, keepdims=True) + EPS)
    return (y * g.astype(jnp.float32)).astype(x.dtype)


def rope(x, pos):
    half = x.shape[-1] // 2
    inv = ROPE_THETA ** (-jnp.arange(half, dtype=jnp.float32) / half)
    ang = pos.astype(jnp.float32)[:, None] * inv[None, :]
    cos = jnp.cos(ang)[:, None, :].astype(x.dtype)
    sin = jnp.sin(ang)[:, None, :].astype(x.dtype)
    x1, x2 = x[..., :half], x[..., half:]
    return jnp.concatenate([x1 * cos - x2 * sin, x2 * cos + x1 * sin], axis=-1)


def dsa_attention(q, k, v, q_idx, k_idx, w_idx):
    B, S, HA, D = q.shape
    topk = min(TOPK_MAX, S // 4)
    nblk = S // BLOCK_Q
    key_pos = jnp.arange(S)
    k_idx32 = k_idx.astype(jnp.float32)

    def one_block(n):
        t0 = n * BLOCK_Q
        qb = lax.dynamic_slice_in_dim(q, t0, BLOCK_Q, axis=1)
        qib = lax.dynamic_slice_in_dim(q_idx, t0, BLOCK_Q, axis=1)
        wb = lax.dynamic_slice_in_dim(w_idx, t0, BLOCK_Q, axis=1)
        qpos = t0 + jnp.arange(BLOCK_Q)
        dots = jnp.einsum('bqhd,bsd->bqhs', qib.astype(jnp.float32), k_idx32) * (IDX_DIM ** -0.5)
        score = jnp.einsum('bqhs,bqh->bqs', jax.nn.relu(dots), wb.astype(jnp.float32))
        admissible = key_pos[None, :] <= qpos[:, None]
        score = jnp.where(admissible[None], score, NEG)
        _, sel = lax.top_k(score, topk)
        valid = sel <= qpos[None, :, None]
        ks = jax.vmap(lambda kb, ib: kb[ib])(k, sel)
        vs = jax.vmap(lambda vb, ib: vb[ib])(v, sel)
        logits = jnp.einsum('bqhd,bqkd->bqhk', qb.astype(jnp.float32), ks.astype(jnp.float32)) * (D ** -0.5)
        logits = jnp.where(valid[:, :, None, :], logits, NEG)
        p = jax.nn.softmax(logits, axis=-1)
        return jnp.einsum('bqhk,bqkd->bqhd', p, vs.astype(jnp.float32)).astype(q.dtype)

    out = lax.map(one_block, jnp.arange(nblk))
    return jnp.moveaxis(out, 0, 1).reshape(B, S, HA, D)


def dilated_branch(q, k, v, window, dilation):
    B, S, H, D = q.shape
    span = window // dilation
    ld = S // dilation
    blk = min(BLOCK_Q, ld)
    nb = -(-ld // blk)
    lp = nb * blk

    def to_sub(a):
        a = a.reshape(B, ld, dilation, H, D).transpose(0, 2, 3, 1, 4)
        a = jnp.pad(a, ((0, 0), (0, 0), (0, 0), (0, lp - ld), (0, 0)))
        return a.reshape(B, dilation, H, nb, blk, D).astype(jnp.float32)

    def with_prev(a):
        prev = jnp.pad(a[:, :, :, :-1], ((0, 0), (0, 0), (0, 0), (1, 0), (0, 0), (0, 0)))
        return jnp.concatenate([prev, a], axis=4)

    qs = to_sub(q)
    kk = with_prev(to_sub(k))
    vv = with_prev(to_sub(v))
    s = jnp.einsum('brhnqd,brhnkd->brhnqk', qs, kk) * (D ** -0.5)
    qi = jnp.arange(blk)[:, None]
    ki = jnp.arange(2 * blk)[None, :]
    dist = blk + qi - ki
    in_band = (dist >= 0) & (dist <= span)
    has_prev = (jnp.arange(nb)[:, None, None] > 0) | (ki >= blk)[None]
    valid = in_band[None] & has_prev
    s = jnp.where(valid, s, NEG)
    m = jnp.max(s, axis=-1, keepdims=True)
    p = jnp.exp(s - m)
    l = jnp.sum(p, axis=-1, keepdims=True)
    o = jnp.einsum('brhnqk,brhnkd->brhnqd', p, vv) / l
    lse = (m + jnp.log(l))[..., 0]
    o = o.reshape(B, dilation, H, lp, D)[:, :, :, :ld].transpose(0, 3, 1, 2, 4).reshape(B, S, H, D)
    lse = lse.reshape(B, dilation, H, lp)[..., :ld].transpose(0, 3, 1, 2).reshape(B, S, H)
    return o, lse


def dilated_attention(q, k, v):
    outs, lses = [], []
    for window, dilation in DILATED_PATTERNS:
        o, lse = dilated_branch(q, k, v, window, dilation)
        outs.append(o)
        lses.append(lse)
    wts = jax.nn.softmax(jnp.stack(lses, axis=0), axis=0)
    out = jnp.einsum('pbsh,pbshd->bshd', wts, jnp.stack(outs, axis=0))
    return out.astype(q.dtype)


def pool_mixer(u, w_pool, scale):
    B, S, C = u.shape
    u32 = u.astype(jnp.float32).reshape(B, S, POOL_GROUPS, POOL_GROUP_DIM)
    csum = jnp.pad(jnp.cumsum(u32, axis=1), ((0, 0), (1, 0), (0, 0), (0, 0)))
    pos = jnp.arange(1, S + 1, dtype=jnp.float32)
    means = []
    for g, w in enumerate(POOL_WINDOWS):
        cg = csum[:, :, g]
        lower = jnp.pad(cg, ((0, 0), (w - 1, 0), (0, 0)))[:, :S]
        count = jnp.minimum(pos, float(w))[None, :, None]
        means.append((cg[:, 1:] - lower) / count)
    y = jnp.stack(means, axis=2) - u32
    y = jnp.einsum('bsgc,gcd->bsgd', y, w_pool.astype(jnp.float32))
    return (y.reshape(B, S, C) * scale.astype(jnp.float32)).astype(u.dtype)


def setup_inputs(seed: int = 0) -> dict:
    key = jax.random.key(seed)
    ks = jax.random.split(key, 13)
    f32 = jnp.float32

    def normal(k, shape, fan_in):
        return jax.random.normal(k, shape, f32) * (fan_in ** -0.5)

    def gain(k, shape):
        return 1.0 + 0.02 * jax.random.normal(k, shape, f32)

    return {
        "x": jax.random.normal(ks[0], (BATCH, SEQ, D_MODEL), f32),
        "g_mix": gain(ks[1], (DEPTH, D_MODEL)),
        "w_in": normal(ks[2], (DEPTH, D_MODEL, IN_COLS), D_MODEL),
        "g_cq": gain(ks[3], (DEPTH, A_Q_RANK)),
        "w_uq": normal(ks[4], (DEPTH, A_Q_RANK, A_WIDTH), A_Q_RANK),
        "w_uq_idx": normal(ks[5], (DEPTH, A_Q_RANK, IDX_HEADS * IDX_DIM), A_Q_RANK),
        "w_pool": normal(ks[6], (DEPTH, POOL_GROUPS, POOL_GROUP_DIM, POOL_GROUP_DIM), POOL_GROUP_DIM),
        "pool_scale": 1.0 + 0.1 * jax.random.normal(ks[7], (DEPTH, C_WIDTH), f32),
        "w_o": normal(ks[8], (DEPTH, MIX_WIDTH, D_MODEL), MIX_WIDTH),
        "g_mlp": gain(ks[9], (DEPTH, D_MODEL)),
        "w_up": normal(ks[10], (DEPTH, D_MODEL, D_FF), D_MODEL),
        "w_down": normal(ks[11], (DEPTH, D_FF, D_MODEL), D_FF),
        "g_final": gain(ks[12], (D_MODEL,)),
    }


def reference(x, g_mix, w_in, g_cq, w_uq, w_uq_idx, w_pool, pool_scale, w_o, g_mlp, w_up, w_down, g_final):
    B, S, _ = x.shape
    pos = jnp.arange(S)
    for l in range(DEPTH):
        h = rms_norm(x, g_mix[l])
        z = h @ w_in[l]
        c_q, k_a, v_a, k_i, w_i, q_b, k_b, v_b, u_c = jnp.split(z, IN_SPLITS, axis=-1)
        c_qn = rms_norm(c_q, g_cq[l])
        q_a = rope((c_qn @ w_uq[l]).reshape(B, S, A_HEADS, HEAD_DIM), pos)
        q_i = rope((c_qn @ w_uq_idx[l]).reshape(B, S, IDX_HEADS, IDX_DIM), pos)
        k_a = rope(k_a[:, :, None, :], pos)[:, :, 0]
        k_i = rope(k_i[:, :, None, :], pos)[:, :, 0]
        o_a = dsa_attention(q_a, k_a, v_a, q_i, k_i, w_i * (IDX_HEADS ** -0.5))
        q_b = rope(q_b.reshape(B, S, B_HEADS, HEAD_DIM), pos)
        k_b = rope(k_b.reshape(B, S, B_HEADS, HEAD_DIM), pos)
        v_b = v_b.reshape(B, S, B_HEADS, HEAD_DIM)
        o_b = dilated_attention(q_b, k_b, v_b)
        o_c = pool_mixer(u_c, w_pool[l], pool_scale[l])
        mix = jnp.concatenate([o_a.reshape(B, S, A_WIDTH), o_b.reshape(B, S, B_WIDTH), o_c], axis=-1)
        x = x + mix @ w_o[l]
        h = rms_norm(x, g_mlp[l])
        x = x + jnp.square(jax.nn.relu(h @ w_up[l])) @ w_down[l]
    return rms_norm(x, g_final)
```

```python
import contextlib
import numpy as np
import concourse.bass as bass
import concourse.mybir as mybir
from concourse.bass_utils import run_bass_kernel_spmd

F32 = mybir.dt.float32
BF16 = mybir.dt.bfloat16
AF = mybir.ActivationFunctionType
ALU = mybir.AluOpType
AX = mybir.AxisListType

D = 2048
DFF = 8192
NCOL = 3664
EPS = 1e-6
SEM_LIMIT = 30000
NI_BISECT = 20
TOPK = 256


class S:
    def __init__(self, k, name, dma=False):
        self.h = k.es.enter_context(k.nc.semaphore(name))
        k.all_sems.append(self)
        self.dma = dma
        self.total = 0
        self.name = name


class Buf:
    __slots__ = ("w", "r", "name")

    def __init__(self, name=""):
        self.w = None
        self.r = {}
        self.name = name


class DmaSem:
    def __init__(self, k, name):
        self.k = k
        self.name = name
        self.s = None
        self.n = 0

    def next(self):
        if self.s is None or self.s.total + 16 > SEM_LIMIT:
            self.s = S(self.k, f"{self.name}_{self.n}", dma=True)
            self.n += 1
        self.s.total += 16
        tok = (self.s, self.s.total)
        self.k.live[id(self.s)] = tok
        return tok


class Eng:
    def __init__(self, k, name, e):
        self.k = k
        self.name = name
        self.e = e
        self.s = None
        self.ns = 0
        self.waited = {}

    def signal(self, inst):
        if self.s is None or self.s.total >= SEM_LIMIT:
            self.s = S(self.k, f"{self.name}_p{self.ns}")
            self.ns += 1
        self.s.total += 1
        inst.then_inc(self.s.h, 1)
        tok = (self.s, self.s.total)
        self.k.live[id(self.s)] = tok
        return tok

    def wait(self, tok):
        if tok is None:
            return
        s, val = tok
        if s.dma:
            val = s.total
        if s is self.s and self.name == "tensor":
            return
        if self.waited.get(id(s), 0) >= val:
            return
        self.e.wait_ge(s.h, val)
        self.waited[id(s)] = val


class T:
    def __init__(self, t, name="", st=None):
        self.t = t
        self.b = Buf(name)
        self.ds = {}
        self.st = st

    def __getitem__(self, key):
        return self.t[key]


class K:
    def __init__(self):
        self.nc = bass.Bass("TRN2", target_bir_lowering=False)
        self.es = contextlib.ExitStack()
        self.es.dsl = []
        self.live = {}
        self.all_sems = []
        self.ds_pool = {"sw": [], "hw": []}
        self.engs = {n: Eng(self, n, getattr(self.nc, n)) for n in ("tensor", "vector", "scalar", "gpsimd", "sync")}
        self.dbufs = {}
        self.uid = 0
        self.ps = None
        self.n_inst = 0

    def name(self, p):
        self.uid += 1
        return f"{p}{self.uid}"

    def dram(self, name, shape, dt, kind="Internal"):
        return self.nc.dram_tensor(name, list(shape), dt, kind=kind).ap()

    def dbuf(self, key):
        b = self.dbufs.get(key)
        if b is None:
            b = self.dbufs[key] = Buf(str(key))
        return b

    def sbuf(self, st, shape, dt, name="t", side=None):
        t = st.enter_context(self.nc.sbuf_tensor(self.name(name), list(shape), dt, side=side))
        return T(t, name, st)

    def dsem(self, tile, queue):
        if isinstance(tile, DmaSem):
            return tile
        kind = "sw" if queue == "gpsimd" else "hw"
        ds = tile.ds.get(kind)
        if ds is None:
            if self.ds_pool[kind]:
                ds = self.ds_pool[kind].pop()
            else:
                ds = DmaSem(self, self.name("d"))
            tile.ds[kind] = ds
            tile.st.dsl.append((kind, ds))
        return ds

    def _pre(self, E, reads, writes):
        for b in reads:
            E.wait(b.w)
        for b in writes:
            E.wait(b.w)
            for t in list(b.r.values()):
                E.wait(t)

    def _post(self, tok, reads, writes):
        for b in reads:
            b.r[id(tok[0])] = tok
        for b in writes:
            b.w = tok
            b.r = {}

    def op(self, eng, fn, reads=(), writes=()):
        E = self.engs[eng]
        reads = [x.b if isinstance(x, T) else x for x in reads]
        writes = [x.b if isinstance(x, T) else x for x in writes]
        self._pre(E, reads, writes)
        inst = fn(E.e)
        tok = E.signal(inst)
        self._post(tok, reads, writes)
        self.n_inst += 1
        return tok

    def dma(self, queue, out, in_, tile, reads=(), writes=(), **kw):
        E = self.engs[queue]
        dsem = self.dsem(tile, queue)
        reads = [x.b if isinstance(x, T) else x for x in reads]
        writes = [x.b if isinstance(x, T) else x for x in writes]
        self._pre(E, reads, writes)
        tok = dsem.next()
        E.e.dma_start(out=out, in_=in_, **kw).then_inc(tok[0].h, 16)
        self._post(tok, reads, writes)
        self.n_inst += 1
        return tok

    def barrier(self):
        toks = list(self.live.values())
        for E in self.engs.values():
            for t in toks:
                E.wait(t)

    @contextlib.contextmanager
    def phase(self):
        st = contextlib.ExitStack()
        st.dsl = []
        with st:
            yield st
            self.barrier()
        for kind, ds in st.dsl:
            self.ds_pool[kind].append(ds)


def merge(gens):
    vt = [0.0] * len(gens)
    alive = list(range(len(gens)))
    while alive:
        gi = min(alive, key=lambda a: vt[a])
        try:
            c = next(gens[gi])
            vt[gi] += (c if c else 1.0)
        except StopIteration:
            alive.remove(gi)


def scaled(gen, f):
    for c in gen:
        yield (c if c else 1.0) * f


def chain(*gens):
    for g in gens:
        yield from g


def drain(gen):
    for _ in gen:
        pass


def load_bcast(k, st, queue, vec_ap, n, name, side):
    t = k.sbuf(st, [128, n], F32, name, side)
    k.dma(queue, t[:], vec_ap.partition_broadcast(128), t, writes=[t])
    return t


def rstd_from_ss(k, ss, rstd, n):
    k.op("scalar", lambda e: e.activation(out=rstd[:, 0:1], in_=ss[:, 0:1], func=AF.Sqrt, bias=EPS, scale=1.0 / n),
         reads=[ss], writes=[rstd])
    k.op("vector", lambda e: e.reciprocal(out=rstd[:, 0:1], in_=rstd[:, 0:1]), reads=[rstd], writes=[rstd])


def xbufs(k, seq, tile, c0=0, c1=8):
    return [k.dbuf(("xres", seq, tile, cb)) for cb in range(c0, c1)]


def norm_transpose_group(k, xres_ap, seq, g, gb, hT, ident, C, pst):
    for i in range(4):
        xt = C["xt"][i % len(C["xt"])]
        hb = C["hb"][i % len(C["hb"])]
        k.dma("sync", xt[:], xres_ap[i * 128:(i + 1) * 128, :], xt, reads=xbufs(k, seq, g * 4 + i), writes=[xt])
        ss, rstd = C["ss"][i % 2], C["rstd"][i % 2]
        k.op("vector", lambda e: e.memset(ss[:, :], 0.0), writes=[ss])
        k.op("scalar", lambda e: e.activation(out=hb[:, :], in_=xt[:, :], func=AF.Square, accum_out=ss[:, 0:1]),
             reads=[xt], writes=[hb, ss])
        rstd_from_ss(k, ss, rstd, D)
        k.op("vector", lambda e: e.scalar_tensor_tensor(out=hb[:, :], in0=xt[:, :], scalar=rstd[:, 0:1], in1=gb[:, :],
                                                        op0=ALU.mult, op1=ALU.mult), reads=[xt, rstd, gb], writes=[hb])
        for half in range(2):
            psv = pst.t[:].bitcast(BF16)

            def tr(e):
                last = None
                for j in range(8):
                    kc = half * 8 + j
                    last = e.transpose(out=psv[:, j * 128:(j + 1) * 128], in_=hb[:, kc * 128:(kc + 1) * 128],
                                       identity=ident[:, :])
                return last
            k.op("tensor", tr, reads=[hb, ident], writes=[pst])
            if half == 0:
                k.op("scalar", lambda e: e.copy(out=hT[:, half * 8:(half + 1) * 8, i * 128:(i + 1) * 128],
                                                in_=psv.rearrange("p (j t) -> p j t", j=8)),
                     reads=[pst], writes=[hT])
            else:
                k.op("vector", lambda e: e.tensor_copy(out=hT[:, half * 8:(half + 1) * 8, i * 128:(i + 1) * 128],
                                                       in_=psv.rearrange("p (j t) -> p j t", j=8)),
                     reads=[pst], writes=[hT])
        yield 2.5


def wslab_load(k, wb, w2d_ap, c0, ncols, nk=16):
    src = w2d_ap.rearrange("(k p) c -> p k c", p=128)[:, :, c0:c0 + ncols]
    k.dma("gpsimd", wb[:, 0:nk * ncols].rearrange("p (k c) -> p k c", k=nk), src, wb, writes=[wb])


def wview(wb, nk, ncols):
    return wb[:, 0:nk * ncols].rearrange("p (k c) -> p k c", k=nk)


def phase_ffn(k, G, l, seq, ngroups, ps, side):
    xres = G["xres"]
    with k.phase() as st:
        sb = lambda shape, dt, nm: k.sbuf(st, shape, dt, nm, side)
        hT = sb([128, 16, 512], BF16, "hT")
        hidT = sb([128, 32, 512], BF16, "hidT")
        wbs = [sb([128, 8192], BF16, "wb") for _ in range(3)]
        C = dict(xt=[sb([128, D], F32, "xt")], hb=[sb([128, D], BF16, "hb")],
                 ss=[sb([128, 1], F32, "ss") for _ in range(2)], rstd=[sb([128, 1], F32, "rstd") for _ in range(2)])
        gb = load_bcast(k, st, "sync", G["g_mlp"][l], D, "gb", side)
        ident = G["ident"]
        tmp = [sb([128, 512], F32, "tmp") for _ in range(2)]
        ot = [sb([128, 256], F32, "ot") for _ in range(4)]
        wup = G["w_up"][l]
        wdn = G["w_down"][l]
        slabs = []
        for half in range(2):
            for s in range(8):
                slabs.append(("up", half, s))
            for cb in range(8):
                slabs.append(("dn", half, cb))
        nsl = len(slabs)
        total = ngroups * nsl

        def issue(idx):
            sl_ = slabs[idx % nsl]
            wb = wbs[idx % 3]
            if sl_[0] == "up":
                wslab_load(k, wb, wup, (sl_[1] * 8 + sl_[2]) * 512, 512, nk=16)
            else:
                wslab_load(k, wb, wdn[sl_[1] * 4096:(sl_[1] + 1) * 4096, :], sl_[2] * 256, 256, nk=32)
        issue(0)
        issue(1)
        idx = 0
        cnt = 0
        ocnt = 0
        pend = [None]
        def norm_gen(g):
            return norm_transpose_group(k, xres[seq, g * 512:(g + 1) * 512, :], seq, g, gb, hT, ident, C, ps[2])
        yield from norm_gen(0)
        for g in range(ngroups):
            nxt = norm_gen(g + 1) if g + 1 < ngroups else None
            for half in range(2):
                for s in range(8):
                    if idx + 2 < total:
                        issue(idx + 2)
                    wv = wview(wbs[idx % 3], 16, 512)
                    wb = wbs[idx % 3]
                    for c in range(4):
                        p_ = ps[cnt % 3]
                        tm = tmp[cnt % 2]
                        cnt += 1

                        def mm(e):
                            last = None
                            for kk in range(16):
                                last = e.matmul(p_[:, :], lhsT=wv[:, kk, c * 128:(c + 1) * 128], rhs=hT[:, kk, :],
                                                start=(kk == 0), stop=(kk == 15))
                            return last
                        k.op("tensor", mm, reads=[wb, hT], writes=[p_])
                        if pend[0] is not None:
                            pend[0]()
                        ff = s * 4 + c

                        def ev(p_=p_, tm=tm, ff=ff):
                            k.op("scalar", lambda e: e.activation(out=tm[:, :], in_=p_[:, :], func=AF.Relu),
                                 reads=[p_], writes=[tm])
                            k.op("vector", lambda e: e.tensor_tensor(out=hidT[:, ff, :], in0=tm[:, :], in1=tm[:, :],
                                                                     op=ALU.mult), reads=[tm], writes=[hidT])
                        pend[0] = ev
                        yield 4.0
                    idx += 1
                if pend[0] is not None:
                    pend[0]()
                    pend[0] = None
                for cb in range(8):
                    if idx + 2 < total:
                        issue(idx + 2)
                    wb = wbs[idx % 3]
                    wv = wview(wb, 32, 256)
                    for i in range(4):
                        p_ = ps[cnt % 3]
                        cnt += 1
                        o = ot[ocnt % 4]
                        ocnt += 1

                        def mm(e):
                            last = None
                            for kk in range(32):
                                last = e.matmul(p_[:, 0:256], lhsT=hidT[:, kk, i * 128:(i + 1) * 128], rhs=wv[:, kk, :],
                                                start=(kk == 0), stop=(kk == 31))
                            return last
                        k.op("tensor", mm, reads=[wb, hidT], writes=[p_])
                        if pend[0] is not None:
                            pend[0]()

                        def ev(p_=p_, o=o, i=i, cb=cb):
                            if i % 2 == 0:
                                k.op("scalar", lambda e: e.copy(out=o[:, :], in_=p_[:, 0:256]), reads=[p_], writes=[o])
                            else:
                                k.op("vector", lambda e: e.tensor_copy(out=o[:, :], in_=p_[:, 0:256]), reads=[p_],
                                     writes=[o])
                            tile = g * 4 + i
                            k.dma("gpsimd", xres[seq, tile * 128:(tile + 1) * 128, cb * 256:(cb + 1) * 256], o[:, :], o,
                                  reads=[o], writes=[k.dbuf(("xres", seq, tile, cb))], accum_op=ALU.add)
                        pend[0] = ev
                        yield 4.2
                    idx += 1
                    if half == 1 and nxt is not None and cb % 2 == 1:
                        if pend[0] is not None:
                            pend[0]()
                            pend[0] = None
                        c_ = next(nxt, None)
                        if c_ is not None:
                            yield c_
                if pend[0] is not None:
                    pend[0]()
                    pend[0] = None
            if nxt is not None:
                yield from nxt


def phase_final(k, G, seq, ntiles, side):
    with k.phase() as st:
        sb = lambda shape, dt, nm: k.sbuf(st, shape, dt, nm, side)
        xt = [sb([128, D], F32, "xt") for _ in range(2)]
        yt = [sb([128, D], F32, "yt") for _ in range(2)]
        ss = [sb([128, 1], F32, "ss") for _ in range(2)]
        rstd = [sb([128, 1], F32, "rstd") for _ in range(2)]
        junk = sb([128, D], BF16, "junk")
        gb = load_bcast(k, st, "sync", G["g_final"], D, "gb", side)
        for i in range(ntiles):
            x, y, s_, r_ = xt[i % 2], yt[i % 2], ss[i % 2], rstd[i % 2]
            k.dma("sync", x[:], G["xres"][seq, i * 128:(i + 1) * 128, :], x, reads=xbufs(k, seq, i), writes=[x])
            k.op("vector", lambda e: e.memset(s_[:, :], 0.0), writes=[s_])
            k.op("scalar", lambda e: e.activation(out=junk[:, :], in_=x[:, :], func=AF.Square, accum_out=s_[:, 0:1]),
                 reads=[x], writes=[junk, s_])
            rstd_from_ss(k, s_, r_, D)
            k.op("vector", lambda e: e.scalar_tensor_tensor(out=y[:, :], in0=x[:, :], scalar=r_[:, 0:1], in1=gb[:, :],
                                                            op0=ALU.mult, op1=ALU.mult), reads=[x, r_, gb], writes=[y])
            k.dma("sync", G["y"][seq, i * 128:(i + 1) * 128, :], y[:], y, reads=[y], writes=[k.dbuf(("y", seq, i))])
            yield 3.0


def w_in_perm():
    idx = []
    idx += list(range(592, 1360))
    idx += list(range(1360, 2128))
    idx += list(range(384, 448))
    idx += list(range(512, 576))
    idx += list(range(2896, 3664))
    idx += list(range(0, 384))
    idx += list(range(448, 512))
    idx += list(range(576, 592))
    idx += list(range(2128, 2896))
    return np.array(idx, dtype=np.int64)


def rope_store(k, G, C, p_, pr, cs, dst_ap, dst_buf, cnt):
    x32 = C["x32"][cnt % 2]
    t1 = C["t1"][cnt % 2]
    t2 = C["t2"][cnt % 2]
    ob = C["ob"][cnt % 2]
    k.op("scalar", lambda e: e.copy(out=x32[:, :], in_=p_[:, :]), reads=[p_], writes=[x32])

    def part2():
        k.op("tensor", lambda e: e.matmul(pr[:, :], lhsT=G["ropeP"][:, :], rhs=x32[:, :], start=True, stop=True),
             reads=[x32, G["ropeP"]], writes=[pr])
        k.op("vector", lambda e: e.tensor_tensor(out=t1[:, :], in0=x32[:, :], in1=cs[:, 0, :], op=ALU.mult),
             reads=[x32, cs], writes=[t1])
        k.op("vector", lambda e: e.tensor_tensor(out=t2[:, :], in0=pr[:, :], in1=cs[:, 1, :], op=ALU.mult),
             reads=[pr, cs], writes=[t2])
        k.op("vector", lambda e: e.tensor_tensor(out=ob[:, :], in0=t1[:, :], in1=t2[:, :], op=ALU.add),
             reads=[t1, t2], writes=[ob])
        k.dma("sync", dst_ap, ob[:, :], ob, reads=[ob], writes=[dst_buf])
    return part2


def phase_in(k, G, l, seq, ngroups, ps, side, nwb=2):
    xres = G["xres"]
    SC = G["sc"][seq]
    with k.phase() as st:
        sb = lambda shape, dt, nm: k.sbuf(st, shape, dt, nm, side)
        hT = sb([128, 16, 512], BF16, "hT")
        wbs = [sb([128, 8192], BF16, "wb") for _ in range(nwb)]
        C = dict(xt=[sb([128, D], F32, "xt")], hb=[sb([128, D], BF16, "hb")],
                 ss=[sb([128, 1], F32, "ss") for _ in range(2)], rstd=[sb([128, 1], F32, "rstd") for _ in range(2)],
                 x32=[sb([128, 512], F32, "x32") for _ in range(2)], t1=[sb([128, 512], F32, "t1") for _ in range(2)],
                 t2=[sb([128, 512], F32, "t2") for _ in range(2)], ob=[sb([128, 512], BF16, "ob") for _ in range(2)])
        gb = load_bcast(k, st, "sync", G["g_mix"][l], D, "gb", side)
        gcq = load_bcast(k, st, "sync", G["g_cq"][l], 384, "gcq", side)
        wuq = sb([128, 3, 1536], BF16, "wuq")
        k.dma("gpsimd", wuq[:, :, 0:512], G["w_uq"][l].rearrange("(k p) c -> p k c", p=128), wuq, writes=[wuq])
        k.dma("gpsimd", wuq[:, :, 512:1536], G["w_uq_idx"][l].rearrange("(k p) c -> p k c", p=128), wuq, writes=[wuq])
        ident = G["ident"]
        css_ = [sb([128, 2, 512], F32, "cs")]
        cqT = sb([128, 3, 512], BF16, "cqT")
        cqn = [sb([128, 384], BF16, "cqn") for _ in range(2)]
        css = [sb([128, 1], F32, "css") for _ in range(2)]
        crs = [sb([128, 1], F32, "crs") for _ in range(2)]
        cjunk = sb([128, 384], BF16, "cjunk")
        va_g = sb([128, 4, 64], BF16, "va_g")
        wi_g = sb([128, 4, 16], F32, "wi_g")
        vb_g = sb([128, 4, 768], BF16, "vb_g")
        uo = [sb([128, 512], F32, "uo")]
        win = G["w_in"][l]
        slabs = [(0, 512), (512, 512), (1024, 512), (1536, 512), (2048, 384), (2432, 464), (2896, 384), (3280, 384)]
        nsl = len(slabs)
        total = ngroups * nsl

        def issue(idx):
            c0, n = slabs[idx % nsl]
            wslab_load(k, wbs[idx % nwb], win, c0, n)
        for j_ in range(nwb - 1):
            issue(j_)
        idx = 0
        cnt = 0
        rcnt = 0
        pend = []

        def flush():
            while pend:
                pend.pop(0)()
        pr = ps[2]
        pst = ps[2]
        for g in range(ngroups):
            tok0 = g * 512
            xg = xres[seq, tok0:tok0 + 512, :]
            cs = css_[0]
            k.dma("sync", cs[:, 0, :], G["cosT_d"][:, tok0:tok0 + 512], cs, writes=[cs])
            k.dma("sync", cs[:, 1, :], G["sinT_d"][:, tok0:tok0 + 512], cs, writes=[cs])
            yield from norm_transpose_group(k, xg, seq, g, gb, hT, ident, C, pst)
            for si in range(5):
                if idx + nwb - 1 < total:
                    issue(idx + nwb - 1)
                wb = wbs[idx % nwb]
                c0, n = slabs[si]
                wv = wview(wb, 16, n)
                for c in range(n // 128):
                    chunk = c0 // 128 + c
                    p_ = ps[cnt % 2]
                    cnt += 1

                    def mm(e):
                        last = None
                        for kk in range(16):
                            last = e.matmul(p_[:, :], lhsT=wv[:, kk, c * 128:(c + 1) * 128], rhs=hT[:, kk, :],
                                            start=(kk == 0), stop=(kk == 15))
                        return last
                    k.op("tensor", mm, reads=[wb, hT], writes=[p_])
                    flush()
                    if chunk < 13:
                        if chunk < 6:
                            name, r0 = "qbT", chunk * 128
                        elif chunk < 12:
                            name, r0 = "kbT", (chunk - 6) * 128
                        else:
                            name, r0 = "kkT", 0
                        pend.append(rope_store(k, G, C, p_, pr, cs, SC[name][r0:r0 + 128, tok0:tok0 + 512],
                                               k.dbuf((name, seq, r0 // 128, g)), rcnt))
                        rcnt += 1
                    else:
                        uc = chunk - 13
                        o = uo[0]
                        k.op("scalar", lambda e: e.copy(out=o[:, :], in_=p_[:, :]), reads=[p_], writes=[o])
                        k.dma("sync", SC["ucT"][uc * 128:(uc + 1) * 128, tok0:tok0 + 512], o[:, :], o,
                              reads=[o], writes=[k.dbuf(("ucT", seq, uc, g))])
                    yield 4.3
                idx += 1
            if idx + nwb - 1 < total:
                issue(idx + nwb - 1)
            wb = wbs[idx % nwb]
            wv = wview(wb, 16, 464)
            for i in range(4):
                p_ = ps[cnt % 2]
                cnt += 1

                def mm(e):
                    last = None
                    for kk in range(16):
                        last = e.matmul(p_[:, 0:464], lhsT=hT[:, kk, i * 128:(i + 1) * 128], rhs=wv[:, kk, :],
                                        start=(kk == 0), stop=(kk == 15))
                    return last
                k.op("tensor", mm, reads=[wb, hT], writes=[p_])
                flush()
                s_, r_, cn = css[i % 2], crs[i % 2], cqn[i % 2]
                k.op("vector", lambda e: e.memset(s_[:, :], 0.0), writes=[s_])
                k.op("scalar", lambda e: e.activation(out=cjunk[:, :], in_=p_[:, 0:384], func=AF.Square,
                                                      accum_out=s_[:, 0:1]), reads=[p_], writes=[cjunk, s_])
                rstd_from_ss(k, s_, r_, 384)
                k.op("vector", lambda e: e.scalar_tensor_tensor(out=cn[:, :], in0=p_[:, 0:384], scalar=r_[:, 0:1],
                                                                in1=gcq[:, :], op0=ALU.mult, op1=ALU.mult),
                     reads=[p_, r_, gcq], writes=[cn])
                k.op("scalar", lambda e: e.copy(out=va_g[:, i, :], in_=p_[:, 384:448]), reads=[p_], writes=[va_g])
                k.op("scalar", lambda e: e.mul(out=wi_g[:, i, :], in_=p_[:, 448:464], mul=1.0 / 32.0), reads=[p_],
                     writes=[wi_g])
                ptv = pst.t[:].bitcast(BF16)

                def tr(e, cn=cn, ptv=ptv):
                    last = None
                    for j in range(3):
                        last = e.transpose(out=ptv[:, j * 128:(j + 1) * 128], in_=cn[:, j * 128:(j + 1) * 128],
                                           identity=ident[:, :])
                    return last
                def trp(tr=tr, cn=cn, i=i, ptv=ptv):
                    k.op("tensor", tr, reads=[cn, ident], writes=[pst])
                    k.op("vector", lambda e: e.tensor_copy(out=cqT[:, :, i * 128:(i + 1) * 128],
                                                           in_=ptv[:, 0:384].rearrange("p (j t) -> p j t", j=3)),
                         reads=[pst], writes=[cqT])
                pend.append(trp)
                yield 4.0
            idx += 1
            k.dma("sync", SC["va"][tok0:tok0 + 512, :].rearrange("(i p) c -> p i c", p=128), va_g[:, :, :],
                  va_g, reads=[va_g], writes=[k.dbuf(("va", seq, g))])
            k.dma("sync", SC["wi"][tok0:tok0 + 512, :].rearrange("(i p) c -> p i c", p=128), wi_g[:, :, :],
                  wi_g, reads=[wi_g], writes=[k.dbuf(("wi", seq, g))])
            for hv in range(2):
                if idx + nwb - 1 < total:
                    issue(idx + nwb - 1)
                wb = wbs[idx % nwb]
                wv = wview(wb, 16, 384)
                for i in range(4):
                    p_ = ps[cnt % 2]
                    cnt += 1

                    def mm(e):
                        last = None
                        for kk in range(16):
                            last = e.matmul(p_[:, 0:384], lhsT=hT[:, kk, i * 128:(i + 1) * 128], rhs=wv[:, kk, :],
                                            start=(kk == 0), stop=(kk == 15))
                        return last
                    k.op("tensor", mm, reads=[wb, hT], writes=[p_])
                    flush()
                    if i % 2 == 0:
                        k.op("scalar", lambda e: e.copy(out=vb_g[:, i, hv * 384:(hv + 1) * 384], in_=p_[:, 0:384]),
                             reads=[p_], writes=[vb_g])
                    else:
                        k.op("vector", lambda e: e.tensor_copy(out=vb_g[:, i, hv * 384:(hv + 1) * 384],
                                                               in_=p_[:, 0:384]), reads=[p_], writes=[vb_g])
                    yield 3.0
                idx += 1
            k.dma("sync", SC["vb"][tok0:tok0 + 512, :].rearrange("(i p) c -> p i c", p=128), vb_g[:, :, :],
                  vb_g, reads=[vb_g], writes=[k.dbuf(("vb", seq, g))])
            flush()
            for c in range(12):
                p_ = ps[cnt % 2]
                cnt += 1

                def mm(e):
                    last = None
                    for kk in range(3):
                        last = e.matmul(p_[:, :], lhsT=wuq[:, kk, c * 128:(c + 1) * 128], rhs=cqT[:, kk, :],
                                        start=(kk == 0), stop=(kk == 2))
                    return last
                k.op("tensor", mm, reads=[wuq, cqT], writes=[p_])
                flush()
                if c < 4:
                    name, r0 = "qaT", c * 128
                else:
                    name, r0 = "qiT", (c - 4) * 128
                pend.append(rope_store(k, G, C, p_, pr, cs, SC[name][r0:r0 + 128, tok0:tok0 + 512],
                                       k.dbuf((name, seq, r0 // 128, g)), rcnt))
                rcnt += 1
                yield 2.0
            flush()


def phase_wo(k, G, l, seq, ngroups, ps, side):
    SC = G["sc"][seq]
    xres = G["xres"]
    with k.phase() as st:
        sb = lambda shape, dt, nm: k.sbuf(st, shape, dt, nm, side)
        mixg = [sb([128, 16, 512], BF16, "mixg") for _ in range(2)]
        wbs = [sb([128, 8192], BF16, "wb") for _ in range(3)]
        ot = [sb([128, 512], F32, "ot") for _ in range(4)]
        wo = G["w_o"][l]
        total = ngroups * 4

        def issue(idx):
            wslab_load(k, wbs[idx % 3], wo, (idx % 4) * 512, 512)
        issue(0)
        issue(1)
        idx = 0
        ocnt = 0
        pend = [None]
        for g in range(ngroups):
            mg = mixg[g % 2]
            rd = [k.dbuf(("mixTa", seq, n)) for n in range(4 * g, 4 * g + 4)]
            rd += [k.dbuf(("mixT", seq, rc)) for rc in range(4, 10)]
            rd += [k.dbuf(("mixTc", seq, gg, m, g)) for gg in range(4) for m in range(2)]
            k.dma("sync", mg[:, :, :], SC["mixT"].rearrange("(k p) t -> p k t", p=128)[:, :, g * 512:(g + 1) * 512],
                  mg, reads=rd, writes=[mg])
            for cg in range(4):
                if idx + 2 < total:
                    issue(idx + 2)
                wb = wbs[idx % 3]
                wv = wview(wb, 16, 512)
                for i in range(4):
                    p_ = ps[ocnt % len(ps)]
                    o = ot[ocnt % 4]
                    ocnt += 1

                    def mm(e):
                        last = None
                        for kk in range(16):
                            last = e.matmul(p_[:, :], lhsT=mg[:, kk, i * 128:(i + 1) * 128], rhs=wv[:, kk, :],
                                            start=(kk == 0), stop=(kk == 15))
                        return last
                    k.op("tensor", mm, reads=[wb, mg], writes=[p_])
                    if pend[0] is not None:
                        pend[0]()

                    def ev(p_=p_, o=o, i=i, cg=cg, g=g):
                        if i % 2 == 0:
                            k.op("scalar", lambda e: e.copy(out=o[:, :], in_=p_[:, :]), reads=[p_], writes=[o])
                        else:
                            k.op("vector", lambda e: e.tensor_copy(out=o[:, :], in_=p_[:, :]), reads=[p_], writes=[o])
                        tile = g * 4 + i
                        k.dma("gpsimd", xres[seq, tile * 128:(tile + 1) * 128, cg * 512:(cg + 1) * 512], o[:, :], o,
                              reads=[o], writes=xbufs(k, seq, tile, cg * 2, cg * 2 + 2), accum_op=ALU.add)
                    pend[0] = ev
                    yield 4.0
                idx += 1
        if pend[0] is not None:
            pend[0]()


def sl(start, n, d):
    return slice(start, start + (n - 1) * d + 1, d)


def band_mask():
    kk = np.arange(128)[:, None]
    qq = np.arange(256)[None, :]
    return ((qq >= kk) & (qq <= kk + 128)).astype(np.float32)


def phase_mixB(k, G, seq, S_, ps, side):
    SC = G["sc"][seq]
    nblk = S_ // 128
    HQ = S_ // 2
    pats = [(1, 0), (4, 1), (16, 2)]
    with k.phase() as st:
        sb = lambda shape, dt, nm: k.sbuf(st, shape, dt, nm, side)
        vb = SC["vb"]
        vreads = [k.dbuf(("vb", seq, g)) for g in range(S_ // 512)]
        vaug = [[sb([128, nblk, 128], BF16, "vaug") for _ in range(3)] for _ in range(2)]
        for a in range(2):
            for o in range(3):
                k.op("vector", lambda e: e.memset(vaug[a][o][:, :, 64:128], 1.0), writes=[vaug[a][o]])
        qT = [sb([64, S_], BF16, "qT") for _ in range(2)]
        kT = [sb([64, S_], BF16, "kT") for _ in range(2)]
        pt = [sb([128, 256], BF16, "pt") for _ in range(8)]
        rc = [sb([64, 512], F32, "rc") for _ in range(2)]
        mixc = [sb([128, S_], BF16, "mixc") for _ in range(2)]
        band = G["band"]
        scnt = 0
        for h in range(12):
            a = h % 2
            q_, k_ = qT[a], kT[a]
            qreads = [k.dbuf(("qbT", seq, h // 2, g)) for g in range(S_ // 512)]
            kreads = [k.dbuf(("kbT", seq, h // 2, g)) for g in range(S_ // 512)]
            k.dma("sync", q_[:, :], SC["qbT"][h * 64:(h + 1) * 64, :], q_, reads=qreads, writes=[q_])
            k.dma("sync", k_[:, :], SC["kbT"][h * 64:(h + 1) * 64, :], k_, reads=kreads, writes=[k_])
            vh = vb[:, h * 64:(h + 1) * 64]
            k.dma("sync", vaug[a][0][:, :, 0:64], vh.rearrange("(j p) c -> p j c", p=128), vaug[a][0], reads=vreads,
                  writes=[vaug[a][0]])
            for r in range(4):
                k.dma("sync", vaug[a][1][:, r * (nblk // 4):(r + 1) * (nblk // 4), 0:64],
                      vh.rearrange("(j p r) c -> p r j c", p=128, r=4)[:, r, :, :], vaug[a][1], reads=vreads,
                      writes=[vaug[a][1]])
            k.dma("sync", vaug[a][2][:, :, 0:64], vh.rearrange("(p r) c -> p r c", r=16), vaug[a][2], reads=vreads,
                  writes=[vaug[a][2]])
            mc = mixc[(h // 2) % 2]
            for hf in range(2):
                for b in range(2):
                    k.op("vector", lambda e: e.memset(ps[b][:, :], 0.0), writes=[ps[b]])
                units = []
                for (d, o) in pats:
                    ld = S_ // d
                    nb = ld // 128
                    P2 = HQ // d
                    for r in range(d):
                        for j in range(nb):
                            qa_ = max(j * 128, hf * P2)
                            qb_ = min(j * 128 + 256, ld, (hf + 1) * P2)
                            if qa_ < qb_:
                                units.append((d, o, r, j, nb, qa_ - j * 128, qb_ - qa_))
                LA = 6
                ptl = {}
                for u in range(len(units) + LA):
                    if u < len(units):
                        d, o, r, j, nb, qlo, nq = units[u]
                        kt0 = j * 128 * d + r
                        qt0 = (j * 128 + qlo) * d + r
                        p_s = ps[2 + scnt % 3]
                        p_ = pt[scnt % 8]
                        scnt += 1
                        ptl[u] = p_
                        k.op("tensor", lambda e: e.matmul(p_s[:, 0:nq], lhsT=k_[:, sl(kt0, 128, d)],
                                                          rhs=q_[:, sl(qt0, nq, d)], start=True, stop=True),
                             reads=[k_, q_], writes=[p_s])
                        k.op("scalar", lambda e: e.activation(out=p_[:, 0:nq], in_=p_s[:, 0:nq], func=AF.Exp,
                                                              scale=0.125), reads=[p_s], writes=[p_])
                        k.op("vector", lambda e: e.tensor_tensor(out=p_[:, 0:nq], in0=p_[:, 0:nq],
                                                                 in1=band[:, qlo:qlo + nq], op=ALU.mult),
                             reads=[p_, band], writes=[p_])
                    if u - LA >= 0:
                        d, o, r, j, nb, qlo, nq = units[u - LA]
                        p_ = ptl.pop(u - LA)
                        qt0 = (j * 128 + qlo) * d + r - hf * HQ
                        blk = r * nb + j
                        va_ = vaug[a][o]
                        qi = 0
                        while qi < nq:
                            tok = qt0 + qi * d
                            bank = tok // 512
                            n = min(nq - qi, (512 * (bank + 1) - tok + d - 1) // d)
                            pb = ps[bank]
                            c0 = tok - bank * 512
                            k.op("tensor", lambda e: e.matmul(pb[:, sl(c0, n, d)], lhsT=va_[:, blk, :],
                                                              rhs=p_[:, qi:qi + n], start=False, stop=True,
                                                              skip_group_check=True),
                                 reads=[va_, p_], writes=[pb])
                            qi += n
                    yield 0.7
                for b in range(2):
                    pb = ps[b]
                    r_ = rc[b % 2]
                    k.op("scalar", lambda e: e.activation(out=r_[:, :], in_=pb[64:128, :], func=AF.Ln), reads=[pb],
                         writes=[r_])
                    k.op("scalar", lambda e: e.activation(out=r_[:, :], in_=r_[:, :], func=AF.Exp, scale=-1.0),
                         reads=[r_], writes=[r_])
                    c0 = hf * HQ + b * 512
                    k.op("vector", lambda e: e.tensor_tensor(out=mc[a * 64:(a + 1) * 64, c0:c0 + 512],
                                                             in0=pb[0:64, :], in1=r_[:, :], op=ALU.mult),
                         reads=[pb, r_], writes=[mc])
                yield 2.0
            if a == 1:
                row0 = 512 + (h // 2) * 128
                k.dma("sync", SC["mixT"][row0:row0 + 128, :], mc[:, :], mc, reads=[mc],
                      writes=[k.dbuf(("mixT", seq, row0 // 128))])


def phase_mixC(k, G, l, seq, S_, ps, side):
    SC = G["sc"][seq]
    PADC = 16
    with k.phase() as st:
        sb = lambda shape, dt, nm: k.sbuf(st, shape, dt, nm, side)
        U = sb([128, PADC + S_], F32, "U")
        A = sb([128, PADC + S_], F32, "A")
        B = sb([128, PADC + S_], F32, "B")
        Yb = [sb([128, S_], BF16, "Yb") for _ in range(2)]
        tmpc = sb([128, 16], F32, "tmpc")
        wp = [sb([128, 192], BF16, "wp0"), sb([64, 192], BF16, "wp1")]
        psc = sb([128, 8], F32, "psc")
        ob = [sb([128, 512], BF16, "obc") for _ in range(2)]
        invc = G["invc"]
        for t in (U, A, B):
            k.op("vector", lambda e: e.memset(t[:, 0:PADC], 0.0), writes=[t])
        for g in range(4):
            for m, (d0, n) in enumerate([(0, 128), (128, 64)]):
                k.dma("sync", psc[0:n, g * 2 + m:g * 2 + m + 1],
                      G["pool_scale"][l, g * 192 + d0:g * 192 + d0 + n].rearrange("(p o) -> p o", o=1), psc,
                      writes=[psc])
        ureads = lambda ch: [k.dbuf(("ucT", seq, ch, gg)) for gg in range(S_ // 512)]
        cnt = 0
        for g in range(4):
            w = 2 ** (g + 1)
            k.dma("gpsimd", wp[0][:, :], G["w_pool"][l, g, 0:128, :], wp[0], writes=[wp[0]])
            k.dma("gpsimd", wp[1][:, :], G["w_pool"][l, g, 128:192, :], wp[1], writes=[wp[1]])
            for m, (c0, n) in enumerate([(0, 128), (128, 64)]):
                r0 = g * 192 + c0
                chs = sorted(set([r0 // 128, (r0 + n - 1) // 128]))
                rd = []
                for ch in chs:
                    rd += ureads(ch)
                k.dma("sync", U[0:n, PADC:PADC + S_], SC["ucT"][r0:r0 + n, :], U, reads=rd, writes=[U])
                src = U
                bufs = [A, B]
                sh = 1
                bi = 0
                while sh < w:
                    dst = bufs[bi % 2]
                    k.op("vector", lambda e: e.tensor_tensor(out=dst[0:n, PADC:PADC + S_], in0=src[0:n, PADC:PADC + S_],
                                                             in1=src[0:n, PADC - sh:PADC - sh + S_], op=ALU.add),
                         reads=[src], writes=[dst])
                    src = dst
                    bi += 1
                    sh *= 2
                    yield 2.2
                Y = Yb[m]
                k.op("vector", lambda e: e.scalar_tensor_tensor(out=Y[0:n, :], in0=src[0:n, PADC:PADC + S_],
                                                                scalar=1.0 / w, in1=U[0:n, PADC:PADC + S_],
                                                                op0=ALU.mult, op1=ALU.subtract),
                     reads=[src, U], writes=[Y])
                k.op("vector", lambda e: e.tensor_tensor(out=tmpc[0:n, 0:w - 1], in0=src[0:n, PADC:PADC + w - 1],
                                                         in1=invc[0:n, 0:w - 1], op=ALU.mult),
                     reads=[src, invc], writes=[tmpc])
                k.op("vector", lambda e: e.tensor_tensor(out=Y[0:n, 0:w - 1], in0=tmpc[0:n, 0:w - 1],
                                                         in1=U[0:n, PADC:PADC + w - 1], op=ALU.subtract),
                     reads=[tmpc, U], writes=[Y])
                yield 2.5
            for m, (d0, n) in enumerate([(0, 128), (128, 64)]):
                for tc in range(S_ // 512):
                    p_ = ps[cnt % len(ps)]
                    o = ob[cnt % 2]
                    cnt += 1

                    def mm(e):
                        e.matmul(p_[0:n, :], lhsT=wp[0][:, d0:d0 + n], rhs=Yb[0][:, tc * 512:(tc + 1) * 512],
                                 start=True, stop=False)
                        return e.matmul(p_[0:n, :], lhsT=wp[1][0:64, d0:d0 + n],
                                        rhs=Yb[1][0:64, tc * 512:(tc + 1) * 512], start=False, stop=True)
                    k.op("tensor", mm, reads=[wp[0], wp[1], Yb[0], Yb[1]], writes=[p_])
                    k.op("scalar", lambda e: e.activation(out=o[0:n, :], in_=p_[0:n, :], func=AF.Copy,
                                                          scale=psc[0:n, g * 2 + m:g * 2 + m + 1]),
                         reads=[p_, psc], writes=[o])
                    row0 = 1280 + g * 192 + d0
                    k.dma("sync", SC["mixT"][row0:row0 + n, tc * 512:(tc + 1) * 512], o[0:n, :], o, reads=[o],
                          writes=[k.dbuf(("mixTc", seq, g, m, tc))])
                    yield 0.7


def caus_neg():
    t = np.arange(128)[:, None]
    s = np.arange(128)[None, :]
    return np.where(s <= t, 0.0, -1e30).astype(np.float32)


def phase_mixA(k, G, seq, S_, ps, side):
    SC = G["sc"][seq]
    nblk = S_ // 128
    ng = S_ // 512
    with k.phase() as st:
        sb = lambda shape, dt, nm: k.sbuf(st, shape, dt, nm, side)
        ki2 = sb([128, S_], BF16, "ki2")
        ka = sb([64, S_], BF16, "ka")
        va_aug = sb([128, nblk, 128], BF16, "va_aug")
        kkreads = [k.dbuf(("kkT", seq, 0, g)) for g in range(ng)]
        k.dma("sync", ki2[0:64, :], SC["kkT"][64:128, :], ki2, reads=kkreads, writes=[ki2])
        k.dma("sync", ki2[64:128, :], SC["kkT"][64:128, :], ki2, reads=kkreads, writes=[ki2])
        k.dma("sync", ka[:, :], SC["kkT"][0:64, :], ka, reads=kkreads, writes=[ka])
        k.op("vector", lambda e: e.memset(va_aug[:, :, 64:128], 1.0), writes=[va_aug])
        k.dma("sync", va_aug[:, :, 0:64], SC["va"].rearrange("(j p) c -> p j c", p=128), va_aug,
              reads=[k.dbuf(("va", seq, g)) for g in range(ng)], writes=[va_aug])
        acc = [sb([128, S_], F32, "acc") for _ in range(2)]
        rr = [sb([128, 512], F32, "rr") for _ in range(2)]
        msk = [sb([128, S_], BF16, "msk")]
        mskT = [sb([128, nblk, 128], BF16, "mskT") for _ in range(2)]
        qi_n = [sb([128, 8, 128], BF16, "qi_n") for _ in range(2)]
        qa_n = [sb([64, 8, 128], BF16, "qa_n") for _ in range(3)]
        wi_n = [sb([128, 16], F32, "wi_n") for _ in range(2)]
        P = [sb([128, 512], BF16, "P") for _ in range(3)]
        oA = [sb([64, 1024], BF16, "oA")]
        rc = [sb([64, 512], F32, "rcA")]
        sc1 = lambda nm: [sb([128, 1], F32, nm) for _ in range(2)]
        mx, rng_, lo, cand, ge = sc1("mx"), sc1("rng"), sc1("lo"), sc1("cand"), sc1("ge")
        sgs = [sb([128, NI_BISECT + 1], F32, "sgs") for _ in range(2)]
        ident = G["ident"]
        causn = G["causn"]
        dc = [0]

        def gen_I(n, i):
            S = (n + 1) * 128
            g = n // 4
            qi, qa, wi = qi_n[i % 2], qa_n[i % 3], wi_n[i % 2]
            a1 = acc[i % 2]
            k.dma("sync", qi[:, :, :], SC["qiT"][:, n * 128:(n + 1) * 128].rearrange("(c p) t -> p c t", p=128),
                  qi, reads=[k.dbuf(("qiT", seq, c, g)) for c in range(8)], writes=[qi])
            k.dma("sync", qa[:, :, :], SC["qaT"][:, n * 128:(n + 1) * 128].rearrange("(h d) t -> d h t", d=64),
                  qa, reads=[k.dbuf(("qaT", seq, c, g)) for c in range(4)], writes=[qa])
            k.dma("sync", wi[:, :], SC["wi"][n * 128:(n + 1) * 128, :], wi, reads=[k.dbuf(("wi", seq, g))],
                  writes=[wi])
            k.op("vector", lambda e: e.memset(a1[:, 0:S], 0.0), writes=[a1])
            nkc = (S + 511) // 512
            for kc in range(nkc):
                wk = min(512, S - kc * 512)
                for head in range(16):
                    c, hh = divmod(head, 2)
                    p_ = ps[dc[0] % 2]
                    r_ = rr[dc[0] % 2]
                    dc[0] += 1
                    k.op("tensor", lambda e: e.matmul(p_[:, 0:wk], lhsT=qi[hh * 64:(hh + 1) * 64, c, :],
                                                      rhs=ki2[hh * 64:(hh + 1) * 64, kc * 512:kc * 512 + wk],
                                                      start=True, stop=True), reads=[qi, ki2], writes=[p_])
                    k.op("scalar", lambda e: e.activation(out=r_[:, 0:wk], in_=p_[:, 0:wk], func=AF.Relu),
                         reads=[p_], writes=[r_])
                    k.op("vector", lambda e: e.scalar_tensor_tensor(out=a1[:, kc * 512:kc * 512 + wk],
                                                                    in0=r_[:, 0:wk], scalar=wi[:, head:head + 1],
                                                                    in1=a1[:, kc * 512:kc * 512 + wk],
                                                                    op0=ALU.mult, op1=ALU.add),
                         reads=[r_, wi, a1], writes=[a1])
                    yield 0.6 * wk / 512
            if S > TOPK:
                l_, m_, r2 = lo[i % 2], mx[i % 2], rng_[i % 2]
                k.op("vector", lambda e: e.tensor_reduce(out=m_[:, 0:1], in_=a1[:, 0:S], axis=AX.X, op=ALU.max),
                     reads=[a1], writes=[m_])
                k.op("vector", lambda e: e.tensor_reduce(out=l_[:, 0:1], in_=a1[:, 0:S], axis=AX.X, op=ALU.min),
                     reads=[a1], writes=[l_])
                k.op("vector", lambda e: e.tensor_tensor(out=r2[:, 0:1], in0=m_[:, 0:1], in1=l_[:, 0:1],
                                                         op=ALU.subtract), reads=[m_, l_], writes=[r2])
            k.op("vector", lambda e: e.tensor_tensor(out=a1[:, n * 128:(n + 1) * 128], in0=a1[:, n * 128:(n + 1) * 128],
                                                     in1=causn[:, :], op=ALU.add), reads=[a1, causn], writes=[a1])
            yield 3.0

        def gen_T(n, i):
            S = (n + 1) * 128
            a1 = acc[i % 2]
            m_ = msk[0]
            mT = mskT[i % 2]
            if S > TOPK:
                l_, r2, c_, g_, sg = lo[i % 2], rng_[i % 2], cand[i % 2], ge[i % 2], sgs[i % 2]
                k.op("vector", lambda e: e.memset(sg[:, :], 0.0), writes=[sg])
                for it in range(1, NI_BISECT + 1):
                    f = 2.0 ** (-it)
                    k.op("vector", lambda e: e.scalar_tensor_tensor(out=c_[:, 0:1], in0=r2[:, 0:1], scalar=-f,
                                                                    in1=l_[:, 0:1], op0=ALU.mult, op1=ALU.subtract),
                         reads=[r2, l_], writes=[c_])
                    k.op("scalar", lambda e: e.activation(out=m_[:, 0:S], in_=a1[:, 0:S], func=AF.Sign,
                                                          bias=c_[:, 0:1], scale=1.0, accum_out=sg[:, it:it + 1]),
                         reads=[a1, c_], writes=[m_, sg])
                    k.op("vector", lambda e: e.tensor_scalar(out=g_[:, 0:1], in0=sg[:, it:it + 1],
                                                             scalar1=float(2 * TOPK - 1 - S), scalar2=f, op0=ALU.is_gt,
                                                             op1=ALU.mult), reads=[sg], writes=[g_])
                    k.op("vector", lambda e: e.scalar_tensor_tensor(out=l_[:, 0:1], in0=g_[:, 0:1],
                                                                    scalar=r2[:, 0:1], in1=l_[:, 0:1],
                                                                    op0=ALU.mult, op1=ALU.add),
                         reads=[g_, r2, l_], writes=[l_])
                    yield 0.3 + 0.8 * S / 1024
                k.op("vector", lambda e: e.tensor_scalar(out=m_[:, 0:S], in0=a1[:, 0:S], scalar1=l_[:, 0:1],
                                                         scalar2=None, op0=ALU.is_gt), reads=[a1, l_], writes=[m_])
            else:
                k.op("vector", lambda e: e.tensor_scalar(out=m_[:, 0:S], in0=a1[:, 0:S], scalar1=-1e29,
                                                         scalar2=None, op0=ALU.is_gt), reads=[a1], writes=[m_])
            yield 1.0
            pt_ = ps[1]
            ptv = pt_.t[:].bitcast(BF16)
            for b0 in range(0, n + 1, 8):
                nb_ = min(8, n + 1 - b0)

                def tr(e):
                    last = None
                    for j in range(nb_):
                        last = e.transpose(out=ptv[:, j * 128:(j + 1) * 128], in_=m_[:, (b0 + j) * 128:(b0 + j + 1) * 128],
                                           identity=ident[:, :])
                    return last
                k.op("tensor", tr, reads=[m_, ident], writes=[pt_])
                k.op("scalar", lambda e: e.copy(out=mT[:, b0:b0 + nb_, :],
                                                in_=ptv[:, 0:nb_ * 128].rearrange("p (j t) -> p j t", j=nb_)),
                     reads=[pt_], writes=[mT])
                yield 1.0

        def gen_X(n, i):
            qa = qa_n[i % 3]
            mT = mskT[i % 2]
            pa = ps[4]
            o_ = oA[0]
            pc = 0
            for hc in range(2):
                def qk(kb, pc_):
                    pS = ps[2 + pc_ % 2]
                    P_ = P[pc_ % 3]
                    k.op("tensor", lambda e: e.matmul(pS[:, :], lhsT=ka[:, kb * 128:(kb + 1) * 128],
                                                      rhs=qa[:, hc * 4:(hc + 1) * 4, :].rearrange("p h t -> p (h t)"),
                                                      start=True, stop=True), reads=[ka, qa], writes=[pS])
                    k.op("scalar", lambda e: e.activation(out=P_[:, :], in_=pS[:, :], func=AF.Exp, scale=0.125),
                         reads=[pS], writes=[P_])
                    k.op("vector", lambda e: e.tensor_tensor(
                        out=P_[:, :].rearrange("p (h t) -> p h t", h=4), in0=P_[:, :].rearrange("p (h t) -> p h t", h=4),
                        in1=mT[:, kb:kb + 1, :].broadcast_to([128, 4, 128]), op=ALU.mult),
                        reads=[P_, mT], writes=[P_])
                    return P_
                pend = qk(0, pc)
                pc += 1
                for kb in range(n + 1):
                    nxt = None
                    if kb + 1 <= n:
                        nxt = qk(kb + 1, pc)
                        pc += 1
                    P_ = pend
                    k.op("tensor", lambda e: e.matmul(pa[:, :], lhsT=va_aug[:, kb, :], rhs=P_[:, :], start=(kb == 0),
                                                      stop=(kb == n)), reads=[va_aug, P_], writes=[pa])
                    pend = nxt
                    yield 0.8
                r_ = rc[0]
                k.op("scalar", lambda e: e.activation(out=r_[:, :], in_=pa[64:128, :], func=AF.Ln), reads=[pa],
                     writes=[r_])
                k.op("scalar", lambda e: e.activation(out=r_[:, :], in_=r_[:, :], func=AF.Exp, scale=-1.0), reads=[r_],
                     writes=[r_])
                k.op("vector", lambda e: e.tensor_tensor(out=o_[:, hc * 512:(hc + 1) * 512], in0=pa[0:64, :],
                                                         in1=r_[:, :], op=ALU.mult), reads=[pa, r_], writes=[o_])
                yield 1.0
            k.dma("sync", SC["mixT"][0:512, n * 128:(n + 1) * 128].rearrange("(h d) t -> d h t", d=64),
                  o_[:, :].rearrange("p (h t) -> p h t", h=8), o_, reads=[o_], writes=[k.dbuf(("mixTa", seq, n))])

        blist = list(range(nblk))
        N = len(blist)
        order = sorted(range(N), key=lambda a: -blist[a])
        for step in range(N + 2):
            gens = []
            if step < N:
                gens.append(gen_I(blist[order[step]], step))
            if 0 <= step - 1 < N:
                gens.append(gen_T(blist[order[step - 1]], step - 1))
            if 0 <= step - 2 < N:
                gens.append(gen_X(blist[order[step - 2]], step - 2))
            vt = [0.0] * len(gens)
            alive = list(range(len(gens)))
            while alive:
                gi = min(alive, key=lambda a_: vt[a_])
                try:
                    c = next(gens[gi])
                    vt[gi] += c
                    yield c
                except StopIteration:
                    alive.remove(gi)


PARAMS = [("g_mix", [D]), ("w_in", [D, NCOL]), ("g_cq", [384]), ("w_uq", [384, 512]), ("w_uq_idx", [384, 1024]),
          ("w_pool", [4, 192, 192]), ("pool_scale", [768]), ("w_o", [D, D]), ("g_mlp", [D]), ("w_up", [D, DFF]),
          ("w_down", [DFF, D])]


def alloc_scratch(k, nseq, S_):
    sc = []
    for s in range(nseq):
        d = {}
        for name, shape, dt in [("qbT", [768, S_], BF16), ("kbT", [768, S_], BF16), ("kkT", [128, S_], BF16),
                                ("ucT", [768, S_], F32), ("qaT", [512, S_], BF16), ("qiT", [1024, S_], BF16),
                                ("va", [S_, 64], BF16), ("vb", [S_, 768], BF16), ("wi", [S_, 16], F32),
                                ("mixT", [2048, S_], BF16)]:
            d[name] = k.dram(f"sc_{name}_{s}", shape, dt)
        sc.append(d)
    return sc


def rope_tables(S_):
    half = 32
    inv = (10000.0 ** (-np.arange(half, dtype=np.float32) / half)).astype(np.float32)
    pos = np.arange(S_, dtype=np.float32)
    ang = pos[None, :] * inv[:, None]
    cosT = np.tile(np.cos(ang).astype(np.float32), (4, 1))
    sinT = np.tile(np.sin(ang).astype(np.float32), (4, 1))
    P = np.zeros((128, 128), np.float32)
    for m in range(128):
        if (m % 64) < 32:
            P[m + 32, m] = -1.0
        else:
            P[m - 32, m] = 1.0
    return cosT, sinT, P


def const_inputs(S_):
    cosT, sinT, P = rope_tables(S_)
    return {"ident": np.eye(128, dtype=np.float32), "cosT": cosT, "sinT": sinT, "ropeP": P, "band": band_mask(),
            "causn": caus_neg(), "invc": np.tile((1.0 / np.arange(1, 17, dtype=np.float32))[None, :], (128, 1))}


def build_full(nlayers, nseq, S_, overlap=True):
    assert nseq == 2
    k = K()
    nc = k.nc
    G = {}
    x = k.dram("x", [nseq, S_, D], F32, kind="ExternalInput")
    for name, shape in PARAMS:
        G[name] = k.dram(name, [nlayers] + shape, F32, kind="ExternalInput")
    G["g_final"] = k.dram("g_final", [D], F32, kind="ExternalInput")
    cd = {n: k.dram(n, s, F32, kind="ExternalInput") for n, s in
          [("ident", [128, 128]), ("cosT", [128, S_]), ("sinT", [128, S_]), ("ropeP", [128, 128]),
           ("band", [128, 256]), ("causn", [128, 128]), ("invc", [128, 16])]}
    G["cosT_d"], G["sinT_d"] = cd["cosT"], cd["sinT"]
    G["y"] = k.dram("y", [nseq, S_, D], F32, kind="ExternalOutput")
    G["xres"] = k.dram("xres", [nseq, S_, D], F32)
    G["sc"] = alloc_scratch(k, nseq, S_)
    ng = S_ // 512
    with k.es:
        k.ps = [T(k.es.enter_context(nc.psum_tensor(f"ps{i}", [128, 512], F32)), f"ps{i}", k.es) for i in range(8)]
        for nm, shape, dt, q in [("ident", [128, 128], BF16, "gpsimd"), ("ropeP", [128, 128], F32, "sync"),
                                 ("band", [128, 256], BF16, "gpsimd"), ("causn", [128, 128], F32, "sync"),
                                 ("invc", [128, 16], F32, "sync")]:
            t = k.sbuf(k.es, shape, dt, nm, "left")
            k.dma(q, t[:], cd[nm][:, :], t, writes=[t])
            G[nm] = t
        cs = DmaSem(k, "cpy")
        for s in range(nseq):
            for i in range(S_ // 128):
                k.dma("sync", G["xres"][s, i * 128:(i + 1) * 128, :], x[s, i * 128:(i + 1) * 128, :], cs,
                      writes=xbufs(k, s, i))
        HP, LP = k.ps[0:3], k.ps[3:8]

        def light(s, l, a_first=True):
            gb_ = scaled(phase_mixB(k, G, s, S_, LP, "right"), 1.1)
            ga_ = scaled(phase_mixA(k, G, s, S_, LP, "right"), 1.9)
            gc_ = scaled(phase_mixC(k, G, l, s, S_, LP, "right"), 2.0)
            return chain(ga_, gb_, gc_) if a_first else chain(gb_, gc_, ga_)

        def h_in(s, l, nwb=3):
            return scaled(phase_in(k, G, l, s, ng, HP, "left", nwb), 1.3)

        def h_out(s, l):
            return chain(phase_wo(k, G, l, s, ng, HP, "left"), phase_ffn(k, G, l, s, ng, HP, "left"))

        if overlap:
            drain(h_in(0, 0))
            for l in range(nlayers):
                hv = ([h_out(1, l - 1)] if l > 0 else []) + [h_in(1, l)]
                merge([chain(*hv), light(0, l, a_first=False)])
                hv = [h_out(0, l)] + ([h_in(0, l + 1)] if l + 1 < nlayers else [])
                merge([chain(*hv), light(1, l, a_first=False)])
            drain(h_out(1, nlayers - 1))
        else:
            for l in range(nlayers):
                for s in range(nseq):
                    drain(chain(h_in(s, l), light(s, l), h_out(s, l)))
        for s in range(nseq):
            drain(phase_final(k, G, s, S_ // 128, "left"))
    return k


N_CORES = 8
_PROG = {}


def kernel(x, g_mix, w_in, g_cq, w_uq, w_uq_idx, w_pool, pool_scale, w_o, g_mlp, w_up, w_down, g_final):
    x = np.asarray(x, dtype=np.float32)
    B, S_, _ = x.shape
    nlayers = int(np.asarray(g_mix).shape[0])
    nseq = B // N_CORES
    key = (nlayers, nseq, S_)
    if key not in _PROG:
        _PROG[key] = build_full(nlayers, nseq, S_)
    k = _PROG[key]
    f = lambda a: np.ascontiguousarray(np.asarray(a, dtype=np.float32))
    shared = {"g_mix": f(g_mix), "w_in": np.ascontiguousarray(np.asarray(w_in, dtype=np.float32)[:, :, w_in_perm()]),
              "g_cq": f(g_cq), "w_uq": f(w_uq), "w_uq_idx": f(w_uq_idx), "w_pool": f(w_pool),
              "pool_scale": f(pool_scale), "w_o": f(w_o), "g_mlp": f(g_mlp), "w_up": f(w_up), "w_down": f(w_down),
              "g_final": f(g_final)}
    shared.update(const_inputs(S_))
    in_maps = []
    for c in range(N_CORES):
        m = dict(shared)
        m["x"] = np.ascontiguousarray(x[c * nseq:(c + 1) * nseq])
        in_maps.append(m)
    res = run_bass_kernel_spmd(k.nc, in_maps, core_ids=list(range(N_CORES)))
    return np.concatenate([np.asarray(r["y"], dtype=np.float32) for r in res.results], axis=0)
```

```python
import contextlib
import numpy as np
import concourse.bass as bass
import concourse.mybir as mybir
from concourse.bass_utils import run_bass_kernel_spmd

F32 = mybir.dt.float32
BF16 = mybir.dt.bfloat16
AF = mybir.ActivationFunctionType
ALU = mybir.AluOpType
AX = mybir.AxisListType

D = 2048
DFF = 8192
NCOL = 3664
EPS = 1e-6
SEM_LIMIT = 30000
NI_BISECT = 20
TOPK = 256


class S:
    def __init__(self, k, name, dma=False):
        self.h = k.es.enter_context(k.nc.semaphore(name))
        k.all_sems.append(self)
        self.dma = dma
        self.total = 0
        self.name = name


class Buf:
    __slots__ = ("w", "r", "name")

    def __init__(self, name=""):
        self.w = None
        self.r = {}
        self.name = name


class DmaSem:
    def __init__(self, k, name):
        self.k = k
        self.name = name
        self.s = None
        self.n = 0

    def next(self):
        if self.s is None or self.s.total + 16 > SEM_LIMIT:
            self.s = S(self.k, f"{self.name}_{self.n}", dma=True)
            self.n += 1
        self.s.total += 16
        tok = (self.s, self.s.total)
        self.k.live[id(self.s)] = tok
        return tok


class Eng:
    def __init__(self, k, name, e):
        self.k = k
        self.name = name
        self.e = e
        self.s = None
        self.ns = 0
        self.waited = {}

    def signal(self, inst):
        if self.s is None or self.s.total >= SEM_LIMIT:
            self.s = S(self.k, f"{self.name}_p{self.ns}")
            self.ns += 1
        self.s.total += 1
        inst.then_inc(self.s.h, 1)
        tok = (self.s, self.s.total)
        self.k.live[id(self.s)] = tok
        return tok

    def wait(self, tok):
        if tok is None:
            return
        s, val = tok
        if s.dma:
            val = s.total
        if s is self.s and self.name == "tensor":
            return
        if self.waited.get(id(s), 0) >= val:
            return
        self.e.wait_ge(s.h, val)
        self.waited[id(s)] = val


class T:
    def __init__(self, t, name="", st=None):
        self.t = t
        self.b = Buf(name)
        self.ds = {}
        self.st = st

    def __getitem__(self, key):
        return self.t[key]


class K:
    def __init__(self):
        self.nc = bass.Bass("TRN2", target_bir_lowering=False)
        self.es = contextlib.ExitStack()
        self.es.dsl = []
        self.live = {}
        self.all_sems = []
        self.ds_pool = {"sw": [], "hw": []}
        self.engs = {n: Eng(self, n, getattr(self.nc, n)) for n in ("tensor", "vector", "scalar", "gpsimd", "sync")}
        self.dbufs = {}
        self.uid = 0
        self.ps = None
        self.n_inst = 0

    def name(self, p):
        self.uid += 1
        return f"{p}{self.uid}"

    def dram(self, name, shape, dt, kind="Internal"):
        return self.nc.dram_tensor(name, list(shape), dt, kind=kind).ap()

    def dbuf(self, key):
        b = self.dbufs.get(key)
        if b is None:
            b = self.dbufs[key] = Buf(str(key))
        return b

    def sbuf(self, st, shape, dt, name="t", side=None):
        t = st.enter_context(self.nc.sbuf_tensor(self.name(name), list(shape), dt, side=side))
        return T(t, name, st)

    def dsem(self, tile, queue):
        if isinstance(tile, DmaSem):
            return tile
        kind = "sw" if queue == "gpsimd" else "hw"
        ds = tile.ds.get(kind)
        if ds is None:
            if self.ds_pool[kind]:
                ds = self.ds_pool[kind].pop()
            else:
                ds = DmaSem(self, self.name("d"))
            tile.ds[kind] = ds
            tile.st.dsl.append((kind, ds))
        return ds

    def _pre(self, E, reads, writes):
        for b in reads:
            E.wait(b.w)
        for b in writes:
            E.wait(b.w)
            for t in list(b.r.values()):
                E.wait(t)

    def _post(self, tok, reads, writes):
        for b in reads:
            b.r[id(tok[0])] = tok
        for b in writes:
            b.w = tok
            b.r = {}

    def op(self, eng, fn, reads=(), writes=()):
        E = self.engs[eng]
        reads = [x.b if isinstance(x, T) else x for x in reads]
        writes = [x.b if isinstance(x, T) else x for x in writes]
        self._pre(E, reads, writes)
        inst = fn(E.e)
        tok = E.signal(inst)
        self._post(tok, reads, writes)
        self.n_inst += 1
        return tok

    def dma(self, queue, out, in_, tile, reads=(), writes=(), **kw):
        E = self.engs[queue]
        dsem = self.dsem(tile, queue)
        reads = [x.b if isinstance(x, T) else x for x in reads]
        writes = [x.b if isinstance(x, T) else x for x in writes]
        self._pre(E, reads, writes)
        tok = dsem.next()
        E.e.dma_start(out=out, in_=in_, **kw).then_inc(tok[0].h, 16)
        self._post(tok, reads, writes)
        self.n_inst += 1
        return tok

    def barrier(self):
        toks = list(self.live.values())
        for E in self.engs.values():
            for t in toks:
                E.wait(t)

    @contextlib.contextmanager
    def phase(self):
        st = contextlib.ExitStack()
        st.dsl = []
        with st:
            yield st
            self.barrier()
        for kind, ds in st.dsl:
            self.ds_pool[kind].append(ds)


def merge(gens):
    vt = [0.0] * len(gens)
    alive = list(range(len(gens)))
    while alive:
        gi = min(alive, key=lambda a: vt[a])
        try:
            c = next(gens[gi])
            vt[gi] += (c if c else 1.0)
        except StopIteration:
            alive.remove(gi)


def scaled(gen, f):
    for c in gen:
        yield (c if c else 1.0) * f


def chain(*gens):
    for g in gens:
        yield from g


def drain(gen):
    for _ in gen:
        pass


def load_bcast(k, st, queue, vec_ap, n, name, side):
    t = k.sbuf(st, [128, n], F32, name, side)
    k.dma(queue, t[:], vec_ap.partition_broadcast(128), t, writes=[t])
    return t


def rstd_from_ss(k, ss, rstd, n):
    k.op("scalar", lambda e: e.activation(out=rstd[:, 0:1], in_=ss[:, 0:1], func=AF.Sqrt, bias=EPS, scale=1.0 / n),
         reads=[ss], writes=[rstd])
    k.op("vector", lambda e: e.reciprocal(out=rstd[:, 0:1], in_=rstd[:, 0:1]), reads=[rstd], writes=[rstd])


def xbufs(k, seq, tile, c0=0, c1=8):
    return [k.dbuf(("xres", seq, tile, cb)) for cb in range(c0, c1)]


def norm_transpose_group(k, xres_ap, seq, g, gb, hT, ident, C, pst):
    for i in range(4):
        xt = C["xt"][i % len(C["xt"])]
        hb = C["hb"][i % len(C["hb"])]
        k.dma("sync", xt[:], xres_ap[i * 128:(i + 1) * 128, :], xt, reads=xbufs(k, seq, g * 4 + i), writes=[xt])
        ss, rstd = C["ss"][i % 2], C["rstd"][i % 2]
        k.op("vector", lambda e: e.memset(ss[:, :], 0.0), writes=[ss])
        k.op("scalar", lambda e: e.activation(out=hb[:, :], in_=xt[:, :], func=AF.Square, accum_out=ss[:, 0:1]),
             reads=[xt], writes=[hb, ss])
        rstd_from_ss(k, ss, rstd, D)
        k.op("vector", lambda e: e.scalar_tensor_tensor(out=hb[:, :], in0=xt[:, :], scalar=rstd[:, 0:1], in1=gb[:, :],
                                                        op0=ALU.mult, op1=ALU.mult), reads=[xt, rstd, gb], writes=[hb])
        for half in range(2):
            psv = pst.t[:].bitcast(BF16)

            def tr(e):
                last = None
                for j in range(8):
                    kc = half * 8 + j
                    last = e.transpose(out=psv[:, j * 128:(j + 1) * 128], in_=hb[:, kc * 128:(kc + 1) * 128],
                                       identity=ident[:, :])
                return last
            k.op("tensor", tr, reads=[hb, ident], writes=[pst])
            if half == 0:
                k.op("scalar", lambda e: e.copy(out=hT[:, half * 8:(half + 1) * 8, i * 128:(i + 1) * 128],
                                                in_=psv.rearrange("p (j t) -> p j t", j=8)),
                     reads=[pst], writes=[hT])
            else:
                k.op("vector", lambda e: e.tensor_copy(out=hT[:, half * 8:(half + 1) * 8, i * 128:(i + 1) * 128],
                                                       in_=psv.rearrange("p (j t) -> p j t", j=8)),
                     reads=[pst], writes=[hT])
        yield 2.5


def wslab_load(k, wb, w2d_ap, c0, ncols, nk=16):
    src = w2d_ap.rearrange("(k p) c -> p k c", p=128)[:, :, c0:c0 + ncols]
    k.dma("gpsimd", wb[:, 0:nk * ncols].rearrange("p (k c) -> p k c", k=nk), src, wb, writes=[wb])


def wview(wb, nk, ncols):
    return wb[:, 0:nk * ncols].rearrange("p (k c) -> p k c", k=nk)


def phase_ffn(k, G, l, seq, ngroups, ps, side):
    xres = G["xres"]
    with k.phase() as st:
        sb = lambda shape, dt, nm: k.sbuf(st, shape, dt, nm, side)
        hT = sb([128, 16, 512], BF16, "hT")
        hidT = sb([128, 32, 512], BF16, "hidT")
        wbs = [sb([128, 8192], BF16, "wb") for _ in range(3)]
        C = dict(xt=[sb([128, D], F32, "xt")], hb=[sb([128, D], BF16, "hb")],
                 ss=[sb([128, 1], F32, "ss") for _ in range(2)], rstd=[sb([128, 1], F32, "rstd") for _ in range(2)])
        gb = load_bcast(k, st, "sync", G["g_mlp"][l], D, "gb", side)
        ident = G["ident"]
        tmp = [sb([128, 512], F32, "tmp") for _ in range(2)]
        ot = [sb([128, 256], F32, "ot") for _ in range(4)]
        wup = G["w_up"][l]
        wdn = G["w_down"][l]
        slabs = []
        for half in range(2):
            for s in range(8):
                slabs.append(("up", half, s))
            for cb in range(8):
                slabs.append(("dn", half, cb))
        nsl = len(slabs)
        total = ngroups * nsl

        def issue(idx):
            sl_ = slabs[idx % nsl]
            wb = wbs[idx % 3]
            if sl_[0] == "up":
                wslab_load(k, wb, wup, (sl_[1] * 8 + sl_[2]) * 512, 512, nk=16)
            else:
                wslab_load(k, wb, wdn[sl_[1] * 4096:(sl_[1] + 1) * 4096, :], sl_[2] * 256, 256, nk=32)
        issue(0)
        issue(1)
        idx = 0
        cnt = 0
        ocnt = 0
        pend = [None]
        def norm_gen(g):
            return norm_transpose_group(k, xres[seq, g * 512:(g + 1) * 512, :], seq, g, gb, hT, ident, C, ps[2])
        yield from norm_gen(0)
        for g in range(ngroups):
            nxt = norm_gen(g + 1) if g + 1 < ngroups else None
            for half in range(2):
                for s in range(8):
                    if idx + 2 < total:
                        issue(idx + 2)
                    wv = wview(wbs[idx % 3], 16, 512)
                    wb = wbs[idx % 3]
                    for c in range(4):
                        p_ = ps[cnt % 3]
                        tm = tmp[cnt % 2]
                        cnt += 1

                        def mm(e):
                            last = None
                            for kk in range(16):
                                last = e.matmul(p_[:, :], lhsT=wv[:, kk, c * 128:(c + 1) * 128], rhs=hT[:, kk, :],
                                                start=(kk == 0), stop=(kk == 15))
                            return last
                        k.op("tensor", mm, reads=[wb, hT], writes=[p_])
                        if pend[0] is not None:
                            pend[0]()
                        ff = s * 4 + c

                        def ev(p_=p_, tm=tm, ff=ff):
                            k.op("scalar", lambda e: e.activation(out=tm[:, :], in_=p_[:, :], func=AF.Relu),
                                 reads=[p_], writes=[tm])
                            k.op("vector", lambda e: e.tensor_tensor(out=hidT[:, ff, :], in0=tm[:, :], in1=tm[:, :],
                                                                     op=ALU.mult), reads=[tm], writes=[hidT])
                        pend[0] = ev
                        yield 4.0
                    idx += 1
                if pend[0] is not None:
                    pend[0]()
                    pend[0] = None
                for cb in range(8):
                    if idx + 2 < total:
                        issue(idx + 2)
                    wb = wbs[idx % 3]
                    wv = wview(wb, 32, 256)
                    for i in range(4):
                        p_ = ps[cnt % 3]
                        cnt += 1
                        o = ot[ocnt % 4]
                        ocnt += 1

                        def mm(e):
                            last = None
                            for kk in range(32):
                                last = e.matmul(p_[:, 0:256], lhsT=hidT[:, kk, i * 128:(i + 1) * 128], rhs=wv[:, kk, :],
                                                start=(kk == 0), stop=(kk == 31))
                            return last
                        k.op("tensor", mm, reads=[wb, hidT], writes=[p_])
                        if pend[0] is not None:
                            pend[0]()

                        def ev(p_=p_, o=o, i=i, cb=cb):
                            if i % 2 == 0:
                                k.op("scalar", lambda e: e.copy(out=o[:, :], in_=p_[:, 0:256]), reads=[p_], writes=[o])
                            else:
                                k.op("vector", lambda e: e.tensor_copy(out=o[:, :], in_=p_[:, 0:256]), reads=[p_],
                                     writes=[o])
                            tile = g * 4 + i
                            k.dma("gpsimd", xres[seq, tile * 128:(tile + 1) * 128, cb * 256:(cb + 1) * 256], o[:, :], o,
                                  reads=[o], writes=[k.dbuf(("xres", seq, tile, cb))], accum_op=ALU.add)
                        pend[0] = ev
                        yield 4.2
                    idx += 1
                    if half == 1 and nxt is not None and cb % 2 == 1:
                        if pend[0] is not None:
                            pend[0]()
                            pend[0] = None
                        c_ = next(nxt, None)
                        if c_ is not None:
                            yield c_
                if pend[0] is not None:
                    pend[0]()
                    pend[0] = None
            if nxt is not None:
                yield from nxt


def phase_final(k, G, seq, ntiles, side):
    with k.phase() as st:
        sb = lambda shape, dt, nm: k.sbuf(st, shape, dt, nm, side)
        xt = [sb([128, D], F32, "xt") for _ in range(2)]
        yt = [sb([128, D], F32, "yt") for _ in range(2)]
        ss = [sb([128, 1], F32, "ss") for _ in range(2)]
        rstd = [sb([128, 1], F32, "rstd") for _ in range(2)]
        junk = sb([128, D], BF16, "junk")
        gb = load_bcast(k, st, "sync", G["g_final"], D, "gb", side)
        for i in range(ntiles):
            x, y, s_, r_ = xt[i % 2], yt[i % 2], ss[i % 2], rstd[i % 2]
            k.dma("sync", x[:], G["xres"][seq, i * 128:(i + 1) * 128, :], x, reads=xbufs(k, seq, i), writes=[x])
            k.op("vector", lambda e: e.memset(s_[:, :], 0.0), writes=[s_])
            k.op("scalar", lambda e: e.activation(out=junk[:, :], in_=x[:, :], func=AF.Square, accum_out=s_[:, 0:1]),
                 reads=[x], writes=[junk, s_])
            rstd_from_ss(k, s_, r_, D)
            k.op("vector", lambda e: e.scalar_tensor_tensor(out=y[:, :], in0=x[:, :], scalar=r_[:, 0:1], in1=gb[:, :],
                                                            op0=ALU.mult, op1=ALU.mult), reads=[x, r_, gb], writes=[y])
            k.dma("sync", G["y"][seq, i * 128:(i + 1) * 128, :], y[:], y, reads=[y], writes=[k.dbuf(("y", seq, i))])
            yield 3.0


def w_in_perm():
    idx = []
    idx += list(range(592, 1360))
    idx += list(range(1360, 2128))
    idx += list(range(384, 448))
    idx += list(range(512, 576))
    idx += list(range(2896, 3664))
    idx += list(range(0, 384))
    idx += list(range(448, 512))
    idx += list(range(576, 592))
    idx += list(range(2128, 2896))
    return np.array(idx, dtype=np.int64)


def rope_store(k, G, C, p_, pr, cs, dst_ap, dst_buf, cnt):
    x32 = C["x32"][cnt % 2]
    t1 = C["t1"][cnt % 2]
    t2 = C["t2"][cnt % 2]
    ob = C["ob"][cnt % 2]
    k.op("scalar", lambda e: e.copy(out=x32[:, :], in_=p_[:, :]), reads=[p_], writes=[x32])

    def part2():
        k.op("tensor", lambda e: e.matmul(pr[:, :], lhsT=G["ropeP"][:, :], rhs=x32[:, :], start=True, stop=True),
             reads=[x32, G["ropeP"]], writes=[pr])
        k.op("vector", lambda e: e.tensor_tensor(out=t1[:, :], in0=x32[:, :], in1=cs[:, 0, :], op=ALU.mult),
             reads=[x32, cs], writes=[t1])
        k.op("vector", lambda e: e.tensor_tensor(out=t2[:, :], in0=pr[:, :], in1=cs[:, 1, :], op=ALU.mult),
             reads=[pr, cs], writes=[t2])
        k.op("vector", lambda e: e.tensor_tensor(out=ob[:, :], in0=t1[:, :], in1=t2[:, :], op=ALU.add),
             reads=[t1, t2], writes=[ob])
        k.dma("sync", dst_ap, ob[:, :], ob, reads=[ob], writes=[dst_buf])
    return part2


def phase_in(k, G, l, seq, ngroups, ps, side):
    xres = G["xres"]
    SC = G["sc"][seq]
    with k.phase() as st:
        sb = lambda shape, dt, nm: k.sbuf(st, shape, dt, nm, side)
        hT = sb([128, 16, 512], BF16, "hT")
        wbs = [sb([128, 8192], BF16, "wb") for _ in range(2)]
        C = dict(xt=[sb([128, D], F32, "xt")], hb=[sb([128, D], BF16, "hb") for _ in range(2)],
                 ss=[sb([128, 1], F32, "ss") for _ in range(2)], rstd=[sb([128, 1], F32, "rstd") for _ in range(2)],
                 x32=[sb([128, 512], F32, "x32") for _ in range(2)], t1=[sb([128, 512], F32, "t1") for _ in range(2)],
                 t2=[sb([128, 512], F32, "t2") for _ in range(2)], ob=[sb([128, 512], BF16, "ob") for _ in range(2)])
        gb = load_bcast(k, st, "sync", G["g_mix"][l], D, "gb", side)
        gcq = load_bcast(k, st, "sync", G["g_cq"][l], 384, "gcq", side)
        wuq = sb([128, 3, 1536], BF16, "wuq")
        k.dma("gpsimd", wuq[:, :, 0:512], G["w_uq"][l].rearrange("(k p) c -> p k c", p=128), wuq, writes=[wuq])
        k.dma("gpsimd", wuq[:, :, 512:1536], G["w_uq_idx"][l].rearrange("(k p) c -> p k c", p=128), wuq, writes=[wuq])
        ident = G["ident"]
        css_ = [sb([128, 2, 512], F32, "cs")]
        cqT = sb([128, 3, 512], BF16, "cqT")
        cqn = [sb([128, 384], BF16, "cqn") for _ in range(2)]
        css = [sb([128, 1], F32, "css") for _ in range(2)]
        crs = [sb([128, 1], F32, "crs") for _ in range(2)]
        cjunk = sb([128, 384], BF16, "cjunk")
        va_g = sb([128, 4, 64], BF16, "va_g")
        wi_g = sb([128, 4, 16], F32, "wi_g")
        vb_g = sb([128, 4, 768], BF16, "vb_g")
        uo = [sb([128, 512], F32, "uo") for _ in range(2)]
        win = G["w_in"][l]
        slabs = [(0, 512), (512, 512), (1024, 512), (1536, 512), (2048, 384), (2432, 464), (2896, 384), (3280, 384)]
        nsl = len(slabs)
        total = ngroups * nsl

        def issue(idx):
            c0, n = slabs[idx % nsl]
            wslab_load(k, wbs[idx % 2], win, c0, n)
        issue(0)
        idx = 0
        cnt = 0
        rcnt = 0
        pend = []

        def flush():
            while pend:
                pend.pop(0)()
        pr = ps[2]
        pst = ps[2]
        for g in range(ngroups):
            tok0 = g * 512
            xg = xres[seq, tok0:tok0 + 512, :]
            cs = css_[0]
            k.dma("sync", cs[:, 0, :], G["cosT_d"][:, tok0:tok0 + 512], cs, writes=[cs])
            k.dma("sync", cs[:, 1, :], G["sinT_d"][:, tok0:tok0 + 512], cs, writes=[cs])
            yield from norm_transpose_group(k, xg, seq, g, gb, hT, ident, C, pst)
            for si in range(5):
                if idx + 1 < total:
                    issue(idx + 1)
                wb = wbs[idx % 2]
                c0, n = slabs[si]
                wv = wview(wb, 16, n)
                for c in range(n // 128):
                    chunk = c0 // 128 + c
                    p_ = ps[cnt % 2]
                    cnt += 1

                    def mm(e):
                        last = None
                        for kk in range(16):
                            last = e.matmul(p_[:, :], lhsT=wv[:, kk, c * 128:(c + 1) * 128], rhs=hT[:, kk, :],
                                            start=(kk == 0), stop=(kk == 15))
                        return last
                    k.op("tensor", mm, reads=[wb, hT], writes=[p_])
                    flush()
                    if chunk < 13:
                        if chunk < 6:
                            name, r0 = "qbT", chunk * 128
                        elif chunk < 12:
                            name, r0 = "kbT", (chunk - 6) * 128
                        else:
                            name, r0 = "kkT", 0
                        pend.append(rope_store(k, G, C, p_, pr, cs, SC[name][r0:r0 + 128, tok0:tok0 + 512],
                                               k.dbuf((name, seq, r0 // 128, g)), rcnt))
                        rcnt += 1
                    else:
                        uc = chunk - 13
                        o = uo[uc % 2]
                        k.op("scalar", lambda e: e.copy(out=o[:, :], in_=p_[:, :]), reads=[p_], writes=[o])
                        k.dma("sync", SC["ucT"][uc * 128:(uc + 1) * 128, tok0:tok0 + 512], o[:, :], o,
                              reads=[o], writes=[k.dbuf(("ucT", seq, uc, g))])
                    yield 4.3
                idx += 1
            if idx + 1 < total:
                issue(idx + 1)
            wb = wbs[idx % 2]
            wv = wview(wb, 16, 464)
            for i in range(4):
                p_ = ps[cnt % 2]
                cnt += 1

                def mm(e):
                    last = None
                    for kk in range(16):
                        last = e.matmul(p_[:, 0:464], lhsT=hT[:, kk, i * 128:(i + 1) * 128], rhs=wv[:, kk, :],
                                        start=(kk == 0), stop=(kk == 15))
                    return last
                k.op("tensor", mm, reads=[wb, hT], writes=[p_])
                flush()
                s_, r_, cn = css[i % 2], crs[i % 2], cqn[i % 2]
                k.op("vector", lambda e: e.memset(s_[:, :], 0.0), writes=[s_])
                k.op("scalar", lambda e: e.activation(out=cjunk[:, :], in_=p_[:, 0:384], func=AF.Square,
                                                      accum_out=s_[:, 0:1]), reads=[p_], writes=[cjunk, s_])
                rstd_from_ss(k, s_, r_, 384)
                k.op("vector", lambda e: e.scalar_tensor_tensor(out=cn[:, :], in0=p_[:, 0:384], scalar=r_[:, 0:1],
                                                                in1=gcq[:, :], op0=ALU.mult, op1=ALU.mult),
                     reads=[p_, r_, gcq], writes=[cn])
                k.op("scalar", lambda e: e.copy(out=va_g[:, i, :], in_=p_[:, 384:448]), reads=[p_], writes=[va_g])
                k.op("scalar", lambda e: e.mul(out=wi_g[:, i, :], in_=p_[:, 448:464], mul=1.0 / 32.0), reads=[p_],
                     writes=[wi_g])
                ptv = pst.t[:].bitcast(BF16)

                def tr(e, cn=cn, ptv=ptv):
                    last = None
                    for j in range(3):
                        last = e.transpose(out=ptv[:, j * 128:(j + 1) * 128], in_=cn[:, j * 128:(j + 1) * 128],
                                           identity=ident[:, :])
                    return last
                def trp(tr=tr, cn=cn, i=i, ptv=ptv):
                    k.op("tensor", tr, reads=[cn, ident], writes=[pst])
                    k.op("vector", lambda e: e.tensor_copy(out=cqT[:, :, i * 128:(i + 1) * 128],
                                                           in_=ptv[:, 0:384].rearrange("p (j t) -> p j t", j=3)),
                         reads=[pst], writes=[cqT])
                pend.append(trp)
                yield 4.0
            idx += 1
            k.dma("sync", SC["va"][tok0:tok0 + 512, :].rearrange("(i p) c -> p i c", p=128), va_g[:, :, :],
                  va_g, reads=[va_g], writes=[k.dbuf(("va", seq, g))])
            k.dma("sync", SC["wi"][tok0:tok0 + 512, :].rearrange("(i p) c -> p i c", p=128), wi_g[:, :, :],
                  wi_g, reads=[wi_g], writes=[k.dbuf(("wi", seq, g))])
            for hv in range(2):
                if idx + 1 < total:
                    issue(idx + 1)
                wb = wbs[idx % 2]
                wv = wview(wb, 16, 384)
                for i in range(4):
                    p_ = ps[cnt % 2]
                    cnt += 1

                    def mm(e):
                        last = None
                        for kk in range(16):
                            last = e.matmul(p_[:, 0:384], lhsT=hT[:, kk, i * 128:(i + 1) * 128], rhs=wv[:, kk, :],
                                            start=(kk == 0), stop=(kk == 15))
                        return last
                    k.op("tensor", mm, reads=[wb, hT], writes=[p_])
                    flush()
                    if i % 2 == 0:
                        k.op("scalar", lambda e: e.copy(out=vb_g[:, i, hv * 384:(hv + 1) * 384], in_=p_[:, 0:384]),
                             reads=[p_], writes=[vb_g])
                    else:
                        k.op("vector", lambda e: e.tensor_copy(out=vb_g[:, i, hv * 384:(hv + 1) * 384],
                                                               in_=p_[:, 0:384]), reads=[p_], writes=[vb_g])
                    yield 3.0
                idx += 1
            k.dma("sync", SC["vb"][tok0:tok0 + 512, :].rearrange("(i p) c -> p i c", p=128), vb_g[:, :, :],
                  vb_g, reads=[vb_g], writes=[k.dbuf(("vb", seq, g))])
            flush()
            for c in range(12):
                p_ = ps[cnt % 2]
                cnt += 1

                def mm(e):
                    last = None
                    for kk in range(3):
                        last = e.matmul(p_[:, :], lhsT=wuq[:, kk, c * 128:(c + 1) * 128], rhs=cqT[:, kk, :],
                                        start=(kk == 0), stop=(kk == 2))
                    return last
                k.op("tensor", mm, reads=[wuq, cqT], writes=[p_])
                flush()
                if c < 4:
                    name, r0 = "qaT", c * 128
                else:
                    name, r0 = "qiT", (c - 4) * 128
                pend.append(rope_store(k, G, C, p_, pr, cs, SC[name][r0:r0 + 128, tok0:tok0 + 512],
                                       k.dbuf((name, seq, r0 // 128, g)), rcnt))
                rcnt += 1
                yield 2.0
            flush()


def phase_wo(k, G, l, seq, ngroups, ps, side):
    SC = G["sc"][seq]
    xres = G["xres"]
    with k.phase() as st:
        sb = lambda shape, dt, nm: k.sbuf(st, shape, dt, nm, side)
        mixg = [sb([128, 16, 512], BF16, "mixg") for _ in range(2)]
        wbs = [sb([128, 8192], BF16, "wb") for _ in range(3)]
        ot = [sb([128, 512], F32, "ot") for _ in range(4)]
        wo = G["w_o"][l]
        total = ngroups * 4

        def issue(idx):
            wslab_load(k, wbs[idx % 3], wo, (idx % 4) * 512, 512)
        issue(0)
        issue(1)
        idx = 0
        ocnt = 0
        pend = [None]
        for g in range(ngroups):
            mg = mixg[g % 2]
            rd = [k.dbuf(("mixTa", seq, n)) for n in range(4 * g, 4 * g + 4)]
            rd += [k.dbuf(("mixT", seq, rc)) for rc in range(4, 10)]
            rd += [k.dbuf(("mixTc", seq, gg, m, g)) for gg in range(4) for m in range(2)]
            k.dma("sync", mg[:, :, :], SC["mixT"].rearrange("(k p) t -> p k t", p=128)[:, :, g * 512:(g + 1) * 512],
                  mg, reads=rd, writes=[mg])
            for cg in range(4):
                if idx + 2 < total:
                    issue(idx + 2)
                wb = wbs[idx % 3]
                wv = wview(wb, 16, 512)
                for i in range(4):
                    p_ = ps[ocnt % len(ps)]
                    o = ot[ocnt % 4]
                    ocnt += 1

                    def mm(e):
                        last = None
                        for kk in range(16):
                            last = e.matmul(p_[:, :], lhsT=mg[:, kk, i * 128:(i + 1) * 128], rhs=wv[:, kk, :],
                                            start=(kk == 0), stop=(kk == 15))
                        return last
                    k.op("tensor", mm, reads=[wb, mg], writes=[p_])
                    if pend[0] is not None:
                        pend[0]()

                    def ev(p_=p_, o=o, i=i, cg=cg, g=g):
                        if i % 2 == 0:
                            k.op("scalar", lambda e: e.copy(out=o[:, :], in_=p_[:, :]), reads=[p_], writes=[o])
                        else:
                            k.op("vector", lambda e: e.tensor_copy(out=o[:, :], in_=p_[:, :]), reads=[p_], writes=[o])
                        tile = g * 4 + i
                        k.dma("gpsimd", xres[seq, tile * 128:(tile + 1) * 128, cg * 512:(cg + 1) * 512], o[:, :], o,
                              reads=[o], writes=xbufs(k, seq, tile, cg * 2, cg * 2 + 2), accum_op=ALU.add)
                    pend[0] = ev
                    yield 4.0
                idx += 1
        if pend[0] is not None:
            pend[0]()


def sl(start, n, d):
    return slice(start, start + (n - 1) * d + 1, d)


def band_mask():
    kk = np.arange(128)[:, None]
    qq = np.arange(256)[None, :]
    return ((qq >= kk) & (qq <= kk + 128)).astype(np.float32)


def phase_mixB(k, G, seq, S_, ps, side):
    SC = G["sc"][seq]
    nblk = S_ // 128
    HQ = S_ // 2
    pats = [(1, 0), (4, 1), (16, 2)]
    with k.phase() as st:
        sb = lambda shape, dt, nm: k.sbuf(st, shape, dt, nm, side)
        vb = SC["vb"]
        vreads = [k.dbuf(("vb", seq, g)) for g in range(S_ // 512)]
        vaug = [[sb([128, nblk, 128], BF16, "vaug") for _ in range(3)] for _ in range(2)]
        for a in range(2):
            for o in range(3):
                k.op("vector", lambda e: e.memset(vaug[a][o][:, :, 64:128], 1.0), writes=[vaug[a][o]])
        qT = [sb([64, S_], BF16, "qT") for _ in range(2)]
        kT = [sb([64, S_], BF16, "kT") for _ in range(2)]
        pt = [sb([128, 256], BF16, "pt") for _ in range(8)]
        rc = [sb([64, 512], F32, "rc") for _ in range(2)]
        mixc = [sb([128, S_], BF16, "mixc") for _ in range(2)]
        band = G["band"]
        scnt = 0
        for h in range(12):
            a = h % 2
            q_, k_ = qT[a], kT[a]
            qreads = [k.dbuf(("qbT", seq, h // 2, g)) for g in range(S_ // 512)]
            kreads = [k.dbuf(("kbT", seq, h // 2, g)) for g in range(S_ // 512)]
            k.dma("sync", q_[:, :], SC["qbT"][h * 64:(h + 1) * 64, :], q_, reads=qreads, writes=[q_])
            k.dma("sync", k_[:, :], SC["kbT"][h * 64:(h + 1) * 64, :], k_, reads=kreads, writes=[k_])
            vh = vb[:, h * 64:(h + 1) * 64]
            k.dma("sync", vaug[a][0][:, :, 0:64], vh.rearrange("(j p) c -> p j c", p=128), vaug[a][0], reads=vreads,
                  writes=[vaug[a][0]])
            for r in range(4):
                k.dma("sync", vaug[a][1][:, r * (nblk // 4):(r + 1) * (nblk // 4), 0:64],
                      vh.rearrange("(j p r) c -> p r j c", p=128, r=4)[:, r, :, :], vaug[a][1], reads=vreads,
                      writes=[vaug[a][1]])
            k.dma("sync", vaug[a][2][:, :, 0:64], vh.rearrange("(p r) c -> p r c", r=16), vaug[a][2], reads=vreads,
                  writes=[vaug[a][2]])
            mc = mixc[(h // 2) % 2]
            for hf in range(2):
                for b in range(2):
                    k.op("vector", lambda e: e.memset(ps[b][:, :], 0.0), writes=[ps[b]])
                units = []
                for (d, o) in pats:
                    ld = S_ // d
                    nb = ld // 128
                    P2 = HQ // d
                    for r in range(d):
                        for j in range(nb):
                            qa_ = max(j * 128, hf * P2)
                            qb_ = min(j * 128 + 256, ld, (hf + 1) * P2)
                            if qa_ < qb_:
                                units.append((d, o, r, j, nb, qa_ - j * 128, qb_ - qa_))
                LA = 6
                ptl = {}
                for u in range(len(units) + LA):
                    if u < len(units):
                        d, o, r, j, nb, qlo, nq = units[u]
                        kt0 = j * 128 * d + r
                        qt0 = (j * 128 + qlo) * d + r
                        p_s = ps[2 + scnt % 3]
                        p_ = pt[scnt % 8]
                        scnt += 1
                        ptl[u] = p_
                        k.op("tensor", lambda e: e.matmul(p_s[:, 0:nq], lhsT=k_[:, sl(kt0, 128, d)],
                                                          rhs=q_[:, sl(qt0, nq, d)], start=True, stop=True),
                             reads=[k_, q_], writes=[p_s])
                        k.op("scalar", lambda e: e.activation(out=p_[:, 0:nq], in_=p_s[:, 0:nq], func=AF.Exp,
                                                              scale=0.125), reads=[p_s], writes=[p_])
                        k.op("vector", lambda e: e.tensor_tensor(out=p_[:, 0:nq], in0=p_[:, 0:nq],
                                                                 in1=band[:, qlo:qlo + nq], op=ALU.mult),
                             reads=[p_, band], writes=[p_])
                    if u - LA >= 0:
                        d, o, r, j, nb, qlo, nq = units[u - LA]
                        p_ = ptl.pop(u - LA)
                        qt0 = (j * 128 + qlo) * d + r - hf * HQ
                        blk = r * nb + j
                        va_ = vaug[a][o]
                        qi = 0
                        while qi < nq:
                            tok = qt0 + qi * d
                            bank = tok // 512
                            n = min(nq - qi, (512 * (bank + 1) - tok + d - 1) // d)
                            pb = ps[bank]
                            c0 = tok - bank * 512
                            k.op("tensor", lambda e: e.matmul(pb[:, sl(c0, n, d)], lhsT=va_[:, blk, :],
                                                              rhs=p_[:, qi:qi + n], start=False, stop=True,
                                                              skip_group_check=True),
                                 reads=[va_, p_], writes=[pb])
                            qi += n
                    yield 0.7
                for b in range(2):
                    pb = ps[b]
                    r_ = rc[b % 2]
                    k.op("scalar", lambda e: e.activation(out=r_[:, :], in_=pb[64:128, :], func=AF.Ln), reads=[pb],
                         writes=[r_])
                    k.op("scalar", lambda e: e.activation(out=r_[:, :], in_=r_[:, :], func=AF.Exp, scale=-1.0),
                         reads=[r_], writes=[r_])
                    c0 = hf * HQ + b * 512
                    k.op("vector", lambda e: e.tensor_tensor(out=mc[a * 64:(a + 1) * 64, c0:c0 + 512],
                                                             in0=pb[0:64, :], in1=r_[:, :], op=ALU.mult),
                         reads=[pb, r_], writes=[mc])
                yield 2.0
            if a == 1:
                row0 = 512 + (h // 2) * 128
                k.dma("sync", SC["mixT"][row0:row0 + 128, :], mc[:, :], mc, reads=[mc],
                      writes=[k.dbuf(("mixT", seq, row0 // 128))])


def phase_mixC(k, G, l, seq, S_, ps, side):
    SC = G["sc"][seq]
    PADC = 16
    with k.phase() as st:
        sb = lambda shape, dt, nm: k.sbuf(st, shape, dt, nm, side)
        U = sb([128, PADC + S_], F32, "U")
        A = sb([128, PADC + S_], F32, "A")
        B = sb([128, PADC + S_], F32, "B")
        Yb = [sb([128, S_], BF16, "Yb") for _ in range(2)]
        tmpc = sb([128, 16], F32, "tmpc")
        wp = [sb([128, 192], BF16, "wp0"), sb([64, 192], BF16, "wp1")]
        psc = sb([128, 8], F32, "psc")
        ob = [sb([128, 512], BF16, "obc") for _ in range(2)]
        invc = G["invc"]
        for t in (U, A, B):
            k.op("vector", lambda e: e.memset(t[:, 0:PADC], 0.0), writes=[t])
        for g in range(4):
            for m, (d0, n) in enumerate([(0, 128), (128, 64)]):
                k.dma("sync", psc[0:n, g * 2 + m:g * 2 + m + 1],
                      G["pool_scale"][l, g * 192 + d0:g * 192 + d0 + n].rearrange("(p o) -> p o", o=1), psc,
                      writes=[psc])
        ureads = lambda ch: [k.dbuf(("ucT", seq, ch, gg)) for gg in range(S_ // 512)]
        cnt = 0
        for g in range(4):
            w = 2 ** (g + 1)
            k.dma("gpsimd", wp[0][:, :], G["w_pool"][l, g, 0:128, :], wp[0], writes=[wp[0]])
            k.dma("gpsimd", wp[1][:, :], G["w_pool"][l, g, 128:192, :], wp[1], writes=[wp[1]])
            for m, (c0, n) in enumerate([(0, 128), (128, 64)]):
                r0 = g * 192 + c0
                chs = sorted(set([r0 // 128, (r0 + n - 1) // 128]))
                rd = []
                for ch in chs:
                    rd += ureads(ch)
                k.dma("sync", U[0:n, PADC:PADC + S_], SC["ucT"][r0:r0 + n, :], U, reads=rd, writes=[U])
                src = U
                bufs = [A, B]
                sh = 1
                bi = 0
                while sh < w:
                    dst = bufs[bi % 2]
                    k.op("vector", lambda e: e.tensor_tensor(out=dst[0:n, PADC:PADC + S_], in0=src[0:n, PADC:PADC + S_],
                                                             in1=src[0:n, PADC - sh:PADC - sh + S_], op=ALU.add),
                         reads=[src], writes=[dst])
                    src = dst
                    bi += 1
                    sh *= 2
                    yield 2.2
                Y = Yb[m]
                k.op("vector", lambda e: e.scalar_tensor_tensor(out=Y[0:n, :], in0=src[0:n, PADC:PADC + S_],
                                                                scalar=1.0 / w, in1=U[0:n, PADC:PADC + S_],
                                                                op0=ALU.mult, op1=ALU.subtract),
                     reads=[src, U], writes=[Y])
                k.op("vector", lambda e: e.tensor_tensor(out=tmpc[0:n, 0:w - 1], in0=src[0:n, PADC:PADC + w - 1],
                                                         in1=invc[0:n, 0:w - 1], op=ALU.mult),
                     reads=[src, invc], writes=[tmpc])
                k.op("vector", lambda e: e.tensor_tensor(out=Y[0:n, 0:w - 1], in0=tmpc[0:n, 0:w - 1],
                                                         in1=U[0:n, PADC:PADC + w - 1], op=ALU.subtract),
                     reads=[tmpc, U], writes=[Y])
                yield 2.5
            for m, (d0, n) in enumerate([(0, 128), (128, 64)]):
                for tc in range(S_ // 512):
                    p_ = ps[cnt % len(ps)]
                    o = ob[cnt % 2]
                    cnt += 1

                    def mm(e):
                        e.matmul(p_[0:n, :], lhsT=wp[0][:, d0:d0 + n], rhs=Yb[0][:, tc * 512:(tc + 1) * 512],
                                 start=True, stop=False)
                        return e.matmul(p_[0:n, :], lhsT=wp[1][0:64, d0:d0 + n],
                                        rhs=Yb[1][0:64, tc * 512:(tc + 1) * 512], start=False, stop=True)
                    k.op("tensor", mm, reads=[wp[0], wp[1], Yb[0], Yb[1]], writes=[p_])
                    k.op("scalar", lambda e: e.activation(out=o[0:n, :], in_=p_[0:n, :], func=AF.Copy,
                                                          scale=psc[0:n, g * 2 + m:g * 2 + m + 1]),
                         reads=[p_, psc], writes=[o])
                    row0 = 1280 + g * 192 + d0
                    k.dma("sync", SC["mixT"][row0:row0 + n, tc * 512:(tc + 1) * 512], o[0:n, :], o, reads=[o],
                          writes=[k.dbuf(("mixTc", seq, g, m, tc))])
                    yield 0.7


def caus_neg():
    t = np.arange(128)[:, None]
    s = np.arange(128)[None, :]
    return np.where(s <= t, 0.0, -1e30).astype(np.float32)


def phase_mixA(k, G, seq, S_, ps, side):
    SC = G["sc"][seq]
    nblk = S_ // 128
    ng = S_ // 512
    with k.phase() as st:
        sb = lambda shape, dt, nm: k.sbuf(st, shape, dt, nm, side)
        ki2 = sb([128, S_], BF16, "ki2")
        ka = sb([64, S_], BF16, "ka")
        va_aug = sb([128, nblk, 128], BF16, "va_aug")
        kkreads = [k.dbuf(("kkT", seq, 0, g)) for g in range(ng)]
        k.dma("sync", ki2[0:64, :], SC["kkT"][64:128, :], ki2, reads=kkreads, writes=[ki2])
        k.dma("sync", ki2[64:128, :], SC["kkT"][64:128, :], ki2, reads=kkreads, writes=[ki2])
        k.dma("sync", ka[:, :], SC["kkT"][0:64, :], ka, reads=kkreads, writes=[ka])
        k.op("vector", lambda e: e.memset(va_aug[:, :, 64:128], 1.0), writes=[va_aug])
        k.dma("sync", va_aug[:, :, 0:64], SC["va"].rearrange("(j p) c -> p j c", p=128), va_aug,
              reads=[k.dbuf(("va", seq, g)) for g in range(ng)], writes=[va_aug])
        acc = [sb([128, S_], F32, "acc") for _ in range(2)]
        rr = [sb([128, 512], F32, "rr") for _ in range(2)]
        msk = [sb([128, S_], BF16, "msk") for _ in range(2)]
        mskT = [sb([128, nblk, 128], BF16, "mskT") for _ in range(2)]
        qi_n = [sb([128, 8, 128], BF16, "qi_n") for _ in range(2)]
        qa_n = [sb([64, 8, 128], BF16, "qa_n") for _ in range(3)]
        wi_n = [sb([128, 16], F32, "wi_n") for _ in range(2)]
        P = [sb([128, 512], BF16, "P") for _ in range(3)]
        oA = [sb([64, 1024], BF16, "oA") for _ in range(2)]
        rc = [sb([64, 512], F32, "rcA")]
        sc1 = lambda nm: [sb([128, 1], F32, nm) for _ in range(2)]
        mx, rng_, lo, cand, ge = sc1("mx"), sc1("rng"), sc1("lo"), sc1("cand"), sc1("ge")
        sgs = [sb([128, NI_BISECT + 1], F32, "sgs") for _ in range(2)]
        ident = G["ident"]
        causn = G["causn"]
        dc = [0]

        def gen_I(n, i):
            S = (n + 1) * 128
            g = n // 4
            qi, qa, wi = qi_n[i % 2], qa_n[i % 3], wi_n[i % 2]
            a1 = acc[i % 2]
            k.dma("sync", qi[:, :, :], SC["qiT"][:, n * 128:(n + 1) * 128].rearrange("(c p) t -> p c t", p=128),
                  qi, reads=[k.dbuf(("qiT", seq, c, g)) for c in range(8)], writes=[qi])
            k.dma("sync", qa[:, :, :], SC["qaT"][:, n * 128:(n + 1) * 128].rearrange("(h d) t -> d h t", d=64),
                  qa, reads=[k.dbuf(("qaT", seq, c, g)) for c in range(4)], writes=[qa])
            k.dma("sync", wi[:, :], SC["wi"][n * 128:(n + 1) * 128, :], wi, reads=[k.dbuf(("wi", seq, g))],
                  writes=[wi])
            k.op("vector", lambda e: e.memset(a1[:, 0:S], 0.0), writes=[a1])
            nkc = (S + 511) // 512
            for kc in range(nkc):
                wk = min(512, S - kc * 512)
                for c in range(8):
                    pp = [ps[0], ps[1]]

                    def mm2(e):
                        e.matmul(pp[0][:, 0:wk], lhsT=qi[0:64, c, :], rhs=ki2[0:64, kc * 512:kc * 512 + wk],
                                 start=True, stop=True)
                        return e.matmul(pp[1][:, 0:wk], lhsT=qi[64:128, c, :], rhs=ki2[64:128, kc * 512:kc * 512 + wk],
                                        start=True, stop=True)
                    k.op("tensor", mm2, reads=[qi, ki2], writes=[pp[0], pp[1]])
                    for hh in range(2):
                        head = 2 * c + hh
                        r_ = rr[hh]
                        p_ = pp[hh]
                        k.op("scalar", lambda e: e.activation(out=r_[:, 0:wk], in_=p_[:, 0:wk], func=AF.Relu),
                             reads=[p_], writes=[r_])
                        k.op("vector", lambda e: e.scalar_tensor_tensor(out=a1[:, kc * 512:kc * 512 + wk],
                                                                        in0=r_[:, 0:wk], scalar=wi[:, head:head + 1],
                                                                        in1=a1[:, kc * 512:kc * 512 + wk],
                                                                        op0=ALU.mult, op1=ALU.add),
                             reads=[r_, wi, a1], writes=[a1])
                    yield 1.2 * wk / 512
            if S > TOPK:
                l_, m_, r2 = lo[i % 2], mx[i % 2], rng_[i % 2]
                k.op("vector", lambda e: e.tensor_reduce(out=m_[:, 0:1], in_=a1[:, 0:S], axis=AX.X, op=ALU.max),
                     reads=[a1], writes=[m_])
                k.op("vector", lambda e: e.tensor_reduce(out=l_[:, 0:1], in_=a1[:, 0:S], axis=AX.X, op=ALU.min),
                     reads=[a1], writes=[l_])
                k.op("vector", lambda e: e.tensor_tensor(out=r2[:, 0:1], in0=m_[:, 0:1], in1=l_[:, 0:1],
                                                         op=ALU.subtract), reads=[m_, l_], writes=[r2])
            k.op("vector", lambda e: e.tensor_tensor(out=a1[:, n * 128:(n + 1) * 128], in0=a1[:, n * 128:(n + 1) * 128],
                                                     in1=causn[:, :], op=ALU.add), reads=[a1, causn], writes=[a1])
            yield 3.0

        def gen_T(n, i):
            S = (n + 1) * 128
            a1 = acc[i % 2]
            m_ = msk[i % 2]
            mT = mskT[i % 2]
            if S > TOPK:
                l_, r2, c_, g_, sg = lo[i % 2], rng_[i % 2], cand[i % 2], ge[i % 2], sgs[i % 2]
                k.op("vector", lambda e: e.memset(sg[:, :], 0.0), writes=[sg])
                for it in range(1, NI_BISECT + 1):
                    f = 2.0 ** (-it)
                    k.op("vector", lambda e: e.scalar_tensor_tensor(out=c_[:, 0:1], in0=r2[:, 0:1], scalar=-f,
                                                                    in1=l_[:, 0:1], op0=ALU.mult, op1=ALU.subtract),
                         reads=[r2, l_], writes=[c_])
                    k.op("scalar", lambda e: e.activation(out=m_[:, 0:S], in_=a1[:, 0:S], func=AF.Sign,
                                                          bias=c_[:, 0:1], scale=1.0, accum_out=sg[:, it:it + 1]),
                         reads=[a1, c_], writes=[m_, sg])
                    k.op("vector", lambda e: e.tensor_scalar(out=g_[:, 0:1], in0=sg[:, it:it + 1],
                                                             scalar1=float(2 * TOPK - 1 - S), scalar2=f, op0=ALU.is_gt,
                                                             op1=ALU.mult), reads=[sg], writes=[g_])
                    k.op("vector", lambda e: e.scalar_tensor_tensor(out=l_[:, 0:1], in0=g_[:, 0:1],
                                                                    scalar=r2[:, 0:1], in1=l_[:, 0:1],
                                                                    op0=ALU.mult, op1=ALU.add),
                         reads=[g_, r2, l_], writes=[l_])
                    yield 0.3 + 0.8 * S / 1024
                k.op("vector", lambda e: e.tensor_scalar(out=m_[:, 0:S], in0=a1[:, 0:S], scalar1=l_[:, 0:1],
                                                         scalar2=None, op0=ALU.is_gt), reads=[a1, l_], writes=[m_])
            else:
                k.op("vector", lambda e: e.tensor_scalar(out=m_[:, 0:S], in0=a1[:, 0:S], scalar1=-1e29,
                                                         scalar2=None, op0=ALU.is_gt), reads=[a1], writes=[m_])
            yield 1.0
            pt_ = ps[1]
            ptv = pt_.t[:].bitcast(BF16)
            for b0 in range(0, n + 1, 8):
                nb_ = min(8, n + 1 - b0)

                def tr(e):
                    last = None
                    for j in range(nb_):
                        last = e.transpose(out=ptv[:, j * 128:(j + 1) * 128], in_=m_[:, (b0 + j) * 128:(b0 + j + 1) * 128],
                                           identity=ident[:, :])
                    return last
                k.op("tensor", tr, reads=[m_, ident], writes=[pt_])
                k.op("scalar", lambda e: e.copy(out=mT[:, b0:b0 + nb_, :],
                                                in_=ptv[:, 0:nb_ * 128].rearrange("p (j t) -> p j t", j=nb_)),
                     reads=[pt_], writes=[mT])
                yield 1.0

        def gen_X(n, i):
            qa = qa_n[i % 3]
            mT = mskT[i % 2]
            pa = ps[4]
            o_ = oA[i % 2]
            pc = 0
            for hc in range(2):
                def qk(kb, pc_):
                    pS = ps[2 + pc_ % 2]
                    P_ = P[pc_ % 3]
                    k.op("tensor", lambda e: e.matmul(pS[:, :], lhsT=ka[:, kb * 128:(kb + 1) * 128],
                                                      rhs=qa[:, hc * 4:(hc + 1) * 4, :].rearrange("p h t -> p (h t)"),
                                                      start=True, stop=True), reads=[ka, qa], writes=[pS])
                    k.op("scalar", lambda e: e.activation(out=P_[:, :], in_=pS[:, :], func=AF.Exp, scale=0.125),
                         reads=[pS], writes=[P_])
                    k.op("vector", lambda e: e.tensor_tensor(
                        out=P_[:, :].rearrange("p (h t) -> p h t", h=4), in0=P_[:, :].rearrange("p (h t) -> p h t", h=4),
                        in1=mT[:, kb:kb + 1, :].broadcast_to([128, 4, 128]), op=ALU.mult),
                        reads=[P_, mT], writes=[P_])
                    return P_
                pend = qk(0, pc)
                pc += 1
                for kb in range(n + 1):
                    nxt = None
                    if kb + 1 <= n:
                        nxt = qk(kb + 1, pc)
                        pc += 1
                    P_ = pend
                    k.op("tensor", lambda e: e.matmul(pa[:, :], lhsT=va_aug[:, kb, :], rhs=P_[:, :], start=(kb == 0),
                                                      stop=(kb == n)), reads=[va_aug, P_], writes=[pa])
                    pend = nxt
                    yield 0.8
                r_ = rc[0]
                k.op("scalar", lambda e: e.activation(out=r_[:, :], in_=pa[64:128, :], func=AF.Ln), reads=[pa],
                     writes=[r_])
                k.op("scalar", lambda e: e.activation(out=r_[:, :], in_=r_[:, :], func=AF.Exp, scale=-1.0), reads=[r_],
                     writes=[r_])
                k.op("vector", lambda e: e.tensor_tensor(out=o_[:, hc * 512:(hc + 1) * 512], in0=pa[0:64, :],
                                                         in1=r_[:, :], op=ALU.mult), reads=[pa, r_], writes=[o_])
                yield 1.0
            k.dma("sync", SC["mixT"][0:512, n * 128:(n + 1) * 128].rearrange("(h d) t -> d h t", d=64),
                  o_[:, :].rearrange("p (h t) -> p h t", h=8), o_, reads=[o_], writes=[k.dbuf(("mixTa", seq, n))])

        blist = list(range(nblk))
        N = len(blist)
        order = sorted(range(N), key=lambda a: -blist[a])
        for step in range(N + 2):
            gens = []
            if step < N:
                gens.append(gen_I(blist[order[step]], step))
            if 0 <= step - 1 < N:
                gens.append(gen_T(blist[order[step - 1]], step - 1))
            if 0 <= step - 2 < N:
                gens.append(gen_X(blist[order[step - 2]], step - 2))
            vt = [0.0] * len(gens)
            alive = list(range(len(gens)))
            while alive:
                gi = min(alive, key=lambda a_: vt[a_])
                try:
                    c = next(gens[gi])
                    vt[gi] += c
                    yield c
                except StopIteration:
                    alive.remove(gi)


PARAMS = [("g_mix", [D]), ("w_in", [D, NCOL]), ("g_cq", [384]), ("w_uq", [384, 512]), ("w_uq_idx", [384, 1024]),
          ("w_pool", [4, 192, 192]), ("pool_scale", [768]), ("w_o", [D, D]), ("g_mlp", [D]), ("w_up", [D, DFF]),
          ("w_down", [DFF, D])]


def alloc_scratch(k, nseq, S_):
    sc = []
    for s in range(nseq):
        d = {}
        for name, shape, dt in [("qbT", [768, S_], BF16), ("kbT", [768, S_], BF16), ("kkT", [128, S_], BF16),
                                ("ucT", [768, S_], F32), ("qaT", [512, S_], BF16), ("qiT", [1024, S_], BF16),
                                ("va", [S_, 64], BF16), ("vb", [S_, 768], BF16), ("wi", [S_, 16], F32),
                                ("mixT", [2048, S_], BF16)]:
            d[name] = k.dram(f"sc_{name}_{s}", shape, dt)
        sc.append(d)
    return sc


def rope_tables(S_):
    half = 32
    inv = (10000.0 ** (-np.arange(half, dtype=np.float32) / half)).astype(np.float32)
    pos = np.arange(S_, dtype=np.float32)
    ang = pos[None, :] * inv[:, None]
    cosT = np.tile(np.cos(ang).astype(np.float32), (4, 1))
    sinT = np.tile(np.sin(ang).astype(np.float32), (4, 1))
    P = np.zeros((128, 128), np.float32)
    for m in range(128):
        if (m % 64) < 32:
            P[m + 32, m] = -1.0
        else:
            P[m - 32, m] = 1.0
    return cosT, sinT, P


def const_inputs(S_):
    cosT, sinT, P = rope_tables(S_)
    return {"ident": np.eye(128, dtype=np.float32), "cosT": cosT, "sinT": sinT, "ropeP": P, "band": band_mask(),
            "causn": caus_neg(), "invc": np.tile((1.0 / np.arange(1, 17, dtype=np.float32))[None, :], (128, 1))}


def build_full(nlayers, nseq, S_, overlap=True):
    assert nseq == 2
    k = K()
    nc = k.nc
    G = {}
    x = k.dram("x", [nseq, S_, D], F32, kind="ExternalInput")
    for name, shape in PARAMS:
        G[name] = k.dram(name, [nlayers] + shape, F32, kind="ExternalInput")
    G["g_final"] = k.dram("g_final", [D], F32, kind="ExternalInput")
    cd = {n: k.dram(n, s, F32, kind="ExternalInput") for n, s in
          [("ident", [128, 128]), ("cosT", [128, S_]), ("sinT", [128, S_]), ("ropeP", [128, 128]),
           ("band", [128, 256]), ("causn", [128, 128]), ("invc", [128, 16])]}
    G["cosT_d"], G["sinT_d"] = cd["cosT"], cd["sinT"]
    G["y"] = k.dram("y", [nseq, S_, D], F32, kind="ExternalOutput")
    G["xres"] = k.dram("xres", [nseq, S_, D], F32)
    G["sc"] = alloc_scratch(k, nseq, S_)
    ng = S_ // 512
    with k.es:
        k.ps = [T(k.es.enter_context(nc.psum_tensor(f"ps{i}", [128, 512], F32)), f"ps{i}", k.es) for i in range(8)]
        for nm, shape, dt, q in [("ident", [128, 128], BF16, "gpsimd"), ("ropeP", [128, 128], F32, "sync"),
                                 ("band", [128, 256], BF16, "gpsimd"), ("causn", [128, 128], F32, "sync"),
                                 ("invc", [128, 16], F32, "sync")]:
            t = k.sbuf(k.es, shape, dt, nm, "left")
            k.dma(q, t[:], cd[nm][:, :], t, writes=[t])
            G[nm] = t
        cs = DmaSem(k, "cpy")
        for s in range(nseq):
            for i in range(S_ // 128):
                k.dma("sync", G["xres"][s, i * 128:(i + 1) * 128, :], x[s, i * 128:(i + 1) * 128, :], cs,
                      writes=xbufs(k, s, i))
        HP, LP = k.ps[0:3], k.ps[3:8]

        def light(s, l):
            return chain(scaled(phase_mixB(k, G, s, S_, LP, "right"), 1.1),
                         scaled(phase_mixA(k, G, s, S_, LP, "right"), 1.9),
                         scaled(phase_mixC(k, G, l, s, S_, LP, "right"), 2.0))

        def h_in(s, l):
            return scaled(phase_in(k, G, l, s, ng, HP, "left"), 1.3)

        def h_out(s, l):
            return chain(phase_wo(k, G, l, s, ng, HP, "left"), phase_ffn(k, G, l, s, ng, HP, "left"))

        if overlap:
            drain(h_in(0, 0))
            for l in range(nlayers):
                hv = ([h_out(1, l - 1)] if l > 0 else []) + [h_in(1, l)]
                merge([chain(*hv), light(0, l)])
                hv = [h_out(0, l)] + ([h_in(0, l + 1)] if l + 1 < nlayers else [])
                merge([chain(*hv), light(1, l)])
            drain(h_out(1, nlayers - 1))
        else:
            for l in range(nlayers):
                for s in range(nseq):
                    drain(chain(h_in(s, l), light(s, l), h_out(s, l)))
        for s in range(nseq):
            drain(phase_final(k, G, s, S_ // 128, "left"))
    return k


N_CORES = 8
_PROG = {}


def kernel(x, g_mix, w_in, g_cq, w_uq, w_uq_idx, w_pool, pool_scale, w_o, g_mlp, w_up, w_down, g_final):
    x = np.asarray(x, dtype=np.float32)
    B, S_, _ = x.shape
    nlayers = int(np.asarray(g_mix).shape[0])
    nseq = B // N_CORES
    key = (nlayers, nseq, S_)
    if key not in _PROG:
        _PROG[key] = build_full(nlayers, nseq, S_)
    k = _PROG[key]
    f = lambda a: np.ascontiguousarray(np.asarray(a, dtype=np.float32))
    shared = {"g_mix": f(g_mix), "w_in": np.ascontiguousarray(np.asarray(w_in, dtype=np.float32)[:, :, w_in_perm()]),
              "g_cq": f(g_cq), "w_uq": f(w_uq), "w_uq_idx": f(w_uq_idx), "w_pool": f(w_pool),
              "pool_scale": f(pool_scale), "w_o": f(w_o), "g_mlp": f(g_mlp), "w_up": f(w_up), "w_down": f(w_down),
              "g_final": f(g_final)}
    shared.update(const_inputs(S_))
    in_maps = []
    for c in range(N_CORES):
        m = dict(shared)
        m["x"] = np.ascontiguousarray(x[c * nseq:(c + 1) * nseq])
        in_maps.append(m)
    res = run_bass_kernel_spmd(k.nc, in_maps, core_ids=list(range(N_CORES)))
    return np.concatenate([np.asarray(r["y"], dtype=np.float32) for r in res.results], axis=0)
```

```python
import contextlib
import numpy as np
import concourse.bass as bass
import concourse.mybir as mybir
from concourse.bass_utils import run_bass_kernel_spmd

F32 = mybir.dt.float32
BF16 = mybir.dt.bfloat16
AF = mybir.ActivationFunctionType
ALU = mybir.AluOpType
AX = mybir.AxisListType

D = 2048
DFF = 8192
NCOL = 3664
EPS = 1e-6
SEM_LIMIT = 30000
NI_BISECT = 16
TOPK = 256


class S:
    def __init__(self, k, name, dma=False):
        self.h = k.es.enter_context(k.nc.semaphore(name))
        k.all_sems.append(self)
        self.dma = dma
        self.total = 0
        self.name = name


class Buf:
    __slots__ = ("w", "r", "name")

    def __init__(self, name=""):
        self.w = None
        self.r = {}
        self.name = name


class DmaSem:
    def __init__(self, k, name):
        self.k = k
        self.name = name
        self.s = None
        self.n = 0

    def next(self):
        if self.s is None or self.s.total + 16 > SEM_LIMIT:
            self.s = S(self.k, f"{self.name}_{self.n}", dma=True)
            self.n += 1
        self.s.total += 16
        tok = (self.s, self.s.total)
        self.k.live[id(self.s)] = tok
        return tok


class Eng:
    def __init__(self, k, name, e):
        self.k = k
        self.name = name
        self.e = e
        self.s = None
        self.ns = 0
        self.waited = {}

    def signal(self, inst):
        if self.s is None or self.s.total >= SEM_LIMIT:
            self.s = S(self.k, f"{self.name}_p{self.ns}")
            self.ns += 1
        self.s.total += 1
        inst.then_inc(self.s.h, 1)
        tok = (self.s, self.s.total)
        self.k.live[id(self.s)] = tok
        return tok

    def wait(self, tok):
        if tok is None:
            return
        s, val = tok
        if s.dma:
            val = s.total
        if s is self.s and self.name == "tensor":
            return
        if self.waited.get(id(s), 0) >= val:
            return
        self.e.wait_ge(s.h, val)
        self.waited[id(s)] = val


class T:
    def __init__(self, t, name="", st=None):
        self.t = t
        self.b = Buf(name)
        self.ds = {}
        self.st = st

    def __getitem__(self, key):
        return self.t[key]


class K:
    def __init__(self):
        self.nc = bass.Bass("TRN2", target_bir_lowering=False)
        self.es = contextlib.ExitStack()
        self.es.dsl = []
        self.live = {}
        self.all_sems = []
        self.ds_pool = {"sw": [], "hw": []}
        self.engs = {n: Eng(self, n, getattr(self.nc, n)) for n in ("tensor", "vector", "scalar", "gpsimd", "sync")}
        self.dbufs = {}
        self.uid = 0
        self.ps = None
        self.n_inst = 0

    def name(self, p):
        self.uid += 1
        return f"{p}{self.uid}"

    def dram(self, name, shape, dt, kind="Internal"):
        return self.nc.dram_tensor(name, list(shape), dt, kind=kind).ap()

    def dbuf(self, key):
        b = self.dbufs.get(key)
        if b is None:
            b = self.dbufs[key] = Buf(str(key))
        return b

    def sbuf(self, st, shape, dt, name="t", side=None):
        t = st.enter_context(self.nc.sbuf_tensor(self.name(name), list(shape), dt, side=side))
        return T(t, name, st)

    def dsem(self, tile, queue):
        if isinstance(tile, DmaSem):
            return tile
        kind = "sw" if queue == "gpsimd" else "hw"
        ds = tile.ds.get(kind)
        if ds is None:
            if self.ds_pool[kind]:
                ds = self.ds_pool[kind].pop()
            else:
                ds = DmaSem(self, self.name("d"))
            tile.ds[kind] = ds
            tile.st.dsl.append((kind, ds))
        return ds

    def _pre(self, E, reads, writes):
        for b in reads:
            E.wait(b.w)
        for b in writes:
            E.wait(b.w)
            for t in list(b.r.values()):
                E.wait(t)

    def _post(self, tok, reads, writes):
        for b in reads:
            b.r[id(tok[0])] = tok
        for b in writes:
            b.w = tok
            b.r = {}

    def op(self, eng, fn, reads=(), writes=()):
        E = self.engs[eng]
        reads = [x.b if isinstance(x, T) else x for x in reads]
        writes = [x.b if isinstance(x, T) else x for x in writes]
        self._pre(E, reads, writes)
        inst = fn(E.e)
        tok = E.signal(inst)
        self._post(tok, reads, writes)
        self.n_inst += 1
        return tok

    def dma(self, queue, out, in_, tile, reads=(), writes=(), **kw):
        E = self.engs[queue]
        dsem = self.dsem(tile, queue)
        reads = [x.b if isinstance(x, T) else x for x in reads]
        writes = [x.b if isinstance(x, T) else x for x in writes]
        self._pre(E, reads, writes)
        tok = dsem.next()
        E.e.dma_start(out=out, in_=in_, **kw).then_inc(tok[0].h, 16)
        self._post(tok, reads, writes)
        self.n_inst += 1
        return tok

    def barrier(self):
        toks = list(self.live.values())
        for E in self.engs.values():
            for t in toks:
                E.wait(t)

    @contextlib.contextmanager
    def phase(self):
        st = contextlib.ExitStack()
        st.dsl = []
        with st:
            yield st
            self.barrier()
        for kind, ds in st.dsl:
            self.ds_pool[kind].append(ds)


def merge(gens):
    vt = [0.0] * len(gens)
    alive = list(range(len(gens)))
    while alive:
        gi = min(alive, key=lambda a: vt[a])
        try:
            c = next(gens[gi])
            vt[gi] += (c if c else 1.0)
        except StopIteration:
            alive.remove(gi)


def scaled(gen, f):
    for c in gen:
        yield (c if c else 1.0) * f


def chain(*gens):
    for g in gens:
        yield from g


def drain(gen):
    for _ in gen:
        pass


def load_bcast(k, st, queue, vec_ap, n, name, side):
    t = k.sbuf(st, [128, n], F32, name, side)
    k.dma(queue, t[:], vec_ap.partition_broadcast(128), t, writes=[t])
    return t


def rstd_from_ss(k, ss, rstd, n):
    k.op("scalar", lambda e: e.activation(out=rstd[:, 0:1], in_=ss[:, 0:1], func=AF.Sqrt, bias=EPS, scale=1.0 / n),
         reads=[ss], writes=[rstd])
    k.op("vector", lambda e: e.reciprocal(out=rstd[:, 0:1], in_=rstd[:, 0:1]), reads=[rstd], writes=[rstd])


def xbufs(k, seq, tile, c0=0, c1=8):
    return [k.dbuf(("xres", seq, tile, cb)) for cb in range(c0, c1)]


def norm_transpose_group(k, xres_ap, seq, g, gb, hT, ident, C, pst):
    for i in range(4):
        xt = C["xt"][i % len(C["xt"])]
        hb = C["hb"][i % len(C["hb"])]
        k.dma("sync", xt[:], xres_ap[i * 128:(i + 1) * 128, :], xt, reads=xbufs(k, seq, g * 4 + i), writes=[xt])
        ss, rstd = C["ss"][i % 2], C["rstd"][i % 2]
        k.op("vector", lambda e: e.memset(ss[:, :], 0.0), writes=[ss])
        k.op("scalar", lambda e: e.activation(out=hb[:, :], in_=xt[:, :], func=AF.Square, accum_out=ss[:, 0:1]),
             reads=[xt], writes=[hb, ss])
        rstd_from_ss(k, ss, rstd, D)
        k.op("vector", lambda e: e.scalar_tensor_tensor(out=hb[:, :], in0=xt[:, :], scalar=rstd[:, 0:1], in1=gb[:, :],
                                                        op0=ALU.mult, op1=ALU.mult), reads=[xt, rstd, gb], writes=[hb])
        for half in range(2):
            psv = pst.t[:].bitcast(BF16)

            def tr(e):
                last = None
                for j in range(8):
                    kc = half * 8 + j
                    last = e.transpose(out=psv[:, j * 128:(j + 1) * 128], in_=hb[:, kc * 128:(kc + 1) * 128],
                                       identity=ident[:, :])
                return last
            k.op("tensor", tr, reads=[hb, ident], writes=[pst])
            if half == 0:
                k.op("scalar", lambda e: e.copy(out=hT[:, half * 8:(half + 1) * 8, i * 128:(i + 1) * 128],
                                                in_=psv.rearrange("p (j t) -> p j t", j=8)),
                     reads=[pst], writes=[hT])
            else:
                k.op("vector", lambda e: e.tensor_copy(out=hT[:, half * 8:(half + 1) * 8, i * 128:(i + 1) * 128],
                                                       in_=psv.rearrange("p (j t) -> p j t", j=8)),
                     reads=[pst], writes=[hT])
        yield 2.5


def wslab_load(k, wb, w2d_ap, c0, ncols, nk=16):
    src = w2d_ap.rearrange("(k p) c -> p k c", p=128)[:, :, c0:c0 + ncols]
    k.dma("gpsimd", wb[:, 0:nk * ncols].rearrange("p (k c) -> p k c", k=nk), src, wb, writes=[wb])


def wview(wb, nk, ncols):
    return wb[:, 0:nk * ncols].rearrange("p (k c) -> p k c", k=nk)


def phase_ffn(k, G, l, seq, ngroups, ps, side):
    xres = G["xres"]
    with k.phase() as st:
        sb = lambda shape, dt, nm: k.sbuf(st, shape, dt, nm, side)
        hT = sb([128, 16, 512], BF16, "hT")
        hidT = sb([128, 32, 512], BF16, "hidT")
        wbs = [sb([128, 8192], BF16, "wb") for _ in range(3)]
        C = dict(xt=[sb([128, D], F32, "xt")], hb=[sb([128, D], BF16, "hb")],
                 ss=[sb([128, 1], F32, "ss") for _ in range(2)], rstd=[sb([128, 1], F32, "rstd") for _ in range(2)])
        gb = load_bcast(k, st, "sync", G["g_mlp"][l], D, "gb", side)
        ident = G["ident"]
        tmp = [sb([128, 512], F32, "tmp") for _ in range(2)]
        ot = [sb([128, 256], F32, "ot") for _ in range(4)]
        wup = G["w_up"][l]
        wdn = G["w_down"][l]
        slabs = []
        for half in range(2):
            for s in range(8):
                slabs.append(("up", half, s))
            for cb in range(8):
                slabs.append(("dn", half, cb))
        nsl = len(slabs)
        total = ngroups * nsl

        def issue(idx):
            sl_ = slabs[idx % nsl]
            wb = wbs[idx % 3]
            if sl_[0] == "up":
                wslab_load(k, wb, wup, (sl_[1] * 8 + sl_[2]) * 512, 512, nk=16)
            else:
                wslab_load(k, wb, wdn[sl_[1] * 4096:(sl_[1] + 1) * 4096, :], sl_[2] * 256, 256, nk=32)
        issue(0)
        issue(1)
        idx = 0
        cnt = 0
        ocnt = 0
        pend = [None]
        def norm_gen(g):
            return norm_transpose_group(k, xres[seq, g * 512:(g + 1) * 512, :], seq, g, gb, hT, ident, C, ps[2])
        yield from norm_gen(0)
        for g in range(ngroups):
            nxt = norm_gen(g + 1) if g + 1 < ngroups else None
            for half in range(2):
                for s in range(8):
                    if idx + 2 < total:
                        issue(idx + 2)
                    wv = wview(wbs[idx % 3], 16, 512)
                    wb = wbs[idx % 3]
                    for c in range(4):
                        p_ = ps[cnt % 3]
                        tm = tmp[cnt % 2]
                        cnt += 1

                        def mm(e):
                            last = None
                            for kk in range(16):
                                last = e.matmul(p_[:, :], lhsT=wv[:, kk, c * 128:(c + 1) * 128], rhs=hT[:, kk, :],
                                                start=(kk == 0), stop=(kk == 15))
                            return last
                        k.op("tensor", mm, reads=[wb, hT], writes=[p_])
                        if pend[0] is not None:
                            pend[0]()
                        ff = s * 4 + c

                        def ev(p_=p_, tm=tm, ff=ff):
                            k.op("scalar", lambda e: e.activation(out=tm[:, :], in_=p_[:, :], func=AF.Relu),
                                 reads=[p_], writes=[tm])
                            k.op("vector", lambda e: e.tensor_tensor(out=hidT[:, ff, :], in0=tm[:, :], in1=tm[:, :],
                                                                     op=ALU.mult), reads=[tm], writes=[hidT])
                        pend[0] = ev
                        yield 4.0
                    idx += 1
                if pend[0] is not None:
                    pend[0]()
                    pend[0] = None
                for cb in range(8):
                    if idx + 2 < total:
                        issue(idx + 2)
                    wb = wbs[idx % 3]
                    wv = wview(wb, 32, 256)
                    for i in range(4):
                        p_ = ps[cnt % 3]
                        cnt += 1
                        o = ot[ocnt % 4]
                        ocnt += 1

                        def mm(e):
                            last = None
                            for kk in range(32):
                                last = e.matmul(p_[:, 0:256], lhsT=hidT[:, kk, i * 128:(i + 1) * 128], rhs=wv[:, kk, :],
                                                start=(kk == 0), stop=(kk == 31))
                            return last
                        k.op("tensor", mm, reads=[wb, hidT], writes=[p_])
                        if pend[0] is not None:
                            pend[0]()

                        def ev(p_=p_, o=o, i=i, cb=cb):
                            if i % 2 == 0:
                                k.op("scalar", lambda e: e.copy(out=o[:, :], in_=p_[:, 0:256]), reads=[p_], writes=[o])
                            else:
                                k.op("vector", lambda e: e.tensor_copy(out=o[:, :], in_=p_[:, 0:256]), reads=[p_],
                                     writes=[o])
                            tile = g * 4 + i
                            k.dma("gpsimd", xres[seq, tile * 128:(tile + 1) * 128, cb * 256:(cb + 1) * 256], o[:, :], o,
                                  reads=[o], writes=[k.dbuf(("xres", seq, tile, cb))], accum_op=ALU.add)
                        pend[0] = ev
                        yield 4.2
                    idx += 1
                    if half == 1 and nxt is not None and cb % 2 == 1:
                        if pend[0] is not None:
                            pend[0]()
                            pend[0] = None
                        c_ = next(nxt, None)
                        if c_ is not None:
                            yield c_
                if pend[0] is not None:
                    pend[0]()
                    pend[0] = None
            if nxt is not None:
                yield from nxt


def phase_final(k, G, seq, ntiles, side):
    with k.phase() as st:
        sb = lambda shape, dt, nm: k.sbuf(st, shape, dt, nm, side)
        xt = [sb([128, D], F32, "xt") for _ in range(2)]
        yt = [sb([128, D], F32, "yt") for _ in range(2)]
        ss = [sb([128, 1], F32, "ss") for _ in range(2)]
        rstd = [sb([128, 1], F32, "rstd") for _ in range(2)]
        junk = sb([128, D], BF16, "junk")
        gb = load_bcast(k, st, "sync", G["g_final"], D, "gb", side)
        for i in range(ntiles):
            x, y, s_, r_ = xt[i % 2], yt[i % 2], ss[i % 2], rstd[i % 2]
            k.dma("sync", x[:], G["xres"][seq, i * 128:(i + 1) * 128, :], x, reads=xbufs(k, seq, i), writes=[x])
            k.op("vector", lambda e: e.memset(s_[:, :], 0.0), writes=[s_])
            k.op("scalar", lambda e: e.activation(out=junk[:, :], in_=x[:, :], func=AF.Square, accum_out=s_[:, 0:1]),
                 reads=[x], writes=[junk, s_])
            rstd_from_ss(k, s_, r_, D)
            k.op("vector", lambda e: e.scalar_tensor_tensor(out=y[:, :], in0=x[:, :], scalar=r_[:, 0:1], in1=gb[:, :],
                                                            op0=ALU.mult, op1=ALU.mult), reads=[x, r_, gb], writes=[y])
            k.dma("sync", G["y"][seq, i * 128:(i + 1) * 128, :], y[:], y, reads=[y], writes=[k.dbuf(("y", seq, i))])
            yield 3.0


def w_in_perm():
    idx = []
    idx += list(range(592, 1360))
    idx += list(range(1360, 2128))
    idx += list(range(384, 448))
    idx += list(range(512, 576))
    idx += list(range(2896, 3664))
    idx += list(range(0, 384))
    idx += list(range(448, 512))
    idx += list(range(576, 592))
    idx += list(range(2128, 2896))
    return np.array(idx, dtype=np.int64)


def rope_store(k, G, C, p_, pr, cs, dst_ap, dst_buf, cnt):
    x32 = C["x32"][cnt % 2]
    t1 = C["t1"][cnt % 2]
    t2 = C["t2"][cnt % 2]
    ob = C["ob"][cnt % 2]
    k.op("scalar", lambda e: e.copy(out=x32[:, :], in_=p_[:, :]), reads=[p_], writes=[x32])

    def part2():
        k.op("tensor", lambda e: e.matmul(pr[:, :], lhsT=G["ropeP"][:, :], rhs=x32[:, :], start=True, stop=True),
             reads=[x32, G["ropeP"]], writes=[pr])
        k.op("vector", lambda e: e.tensor_tensor(out=t1[:, :], in0=x32[:, :], in1=cs[:, 0, :], op=ALU.mult),
             reads=[x32, cs], writes=[t1])
        k.op("vector", lambda e: e.tensor_tensor(out=t2[:, :], in0=pr[:, :], in1=cs[:, 1, :], op=ALU.mult),
             reads=[pr, cs], writes=[t2])
        k.op("vector", lambda e: e.tensor_tensor(out=ob[:, :], in0=t1[:, :], in1=t2[:, :], op=ALU.add),
             reads=[t1, t2], writes=[ob])
        k.dma("sync", dst_ap, ob[:, :], ob, reads=[ob], writes=[dst_buf])
    return part2


def phase_in(k, G, l, seq, ngroups, ps, side):
    xres = G["xres"]
    SC = G["sc"][seq]
    with k.phase() as st:
        sb = lambda shape, dt, nm: k.sbuf(st, shape, dt, nm, side)
        hT = sb([128, 16, 512], BF16, "hT")
        wbs = [sb([128, 8192], BF16, "wb") for _ in range(2)]
        C = dict(xt=[sb([128, D], F32, "xt")], hb=[sb([128, D], BF16, "hb") for _ in range(2)],
                 ss=[sb([128, 1], F32, "ss") for _ in range(2)], rstd=[sb([128, 1], F32, "rstd") for _ in range(2)],
                 x32=[sb([128, 512], F32, "x32") for _ in range(2)], t1=[sb([128, 512], F32, "t1") for _ in range(2)],
                 t2=[sb([128, 512], F32, "t2") for _ in range(2)], ob=[sb([128, 512], BF16, "ob") for _ in range(2)])
        gb = load_bcast(k, st, "sync", G["g_mix"][l], D, "gb", side)
        gcq = load_bcast(k, st, "sync", G["g_cq"][l], 384, "gcq", side)
        wuq = sb([128, 3, 1536], BF16, "wuq")
        k.dma("gpsimd", wuq[:, :, 0:512], G["w_uq"][l].rearrange("(k p) c -> p k c", p=128), wuq, writes=[wuq])
        k.dma("gpsimd", wuq[:, :, 512:1536], G["w_uq_idx"][l].rearrange("(k p) c -> p k c", p=128), wuq, writes=[wuq])
        ident = G["ident"]
        css_ = [sb([128, 2, 512], F32, "cs")]
        cqT = sb([128, 3, 512], BF16, "cqT")
        cqn = [sb([128, 384], BF16, "cqn") for _ in range(2)]
        css = [sb([128, 1], F32, "css") for _ in range(2)]
        crs = [sb([128, 1], F32, "crs") for _ in range(2)]
        cjunk = sb([128, 384], BF16, "cjunk")
        va_g = sb([128, 4, 64], BF16, "va_g")
        wi_g = sb([128, 4, 16], F32, "wi_g")
        vb_g = sb([128, 4, 768], BF16, "vb_g")
        uo = [sb([128, 512], F32, "uo") for _ in range(2)]
        win = G["w_in"][l]
        slabs = [(0, 512), (512, 512), (1024, 512), (1536, 512), (2048, 384), (2432, 464), (2896, 384), (3280, 384)]
        nsl = len(slabs)
        total = ngroups * nsl

        def issue(idx):
            c0, n = slabs[idx % nsl]
            wslab_load(k, wbs[idx % 2], win, c0, n)
        issue(0)
        idx = 0
        cnt = 0
        rcnt = 0
        pend = []

        def flush():
            while pend:
                pend.pop(0)()
        pr = ps[2]
        pst = ps[2]
        for g in range(ngroups):
            tok0 = g * 512
            xg = xres[seq, tok0:tok0 + 512, :]
            cs = css_[0]
            k.dma("sync", cs[:, 0, :], G["cosT_d"][:, tok0:tok0 + 512], cs, writes=[cs])
            k.dma("sync", cs[:, 1, :], G["sinT_d"][:, tok0:tok0 + 512], cs, writes=[cs])
            yield from norm_transpose_group(k, xg, seq, g, gb, hT, ident, C, pst)
            for si in range(5):
                if idx + 1 < total:
                    issue(idx + 1)
                wb = wbs[idx % 2]
                c0, n = slabs[si]
                wv = wview(wb, 16, n)
                for c in range(n // 128):
                    chunk = c0 // 128 + c
                    p_ = ps[cnt % 2]
                    cnt += 1

                    def mm(e):
                        last = None
                        for kk in range(16):
                            last = e.matmul(p_[:, :], lhsT=wv[:, kk, c * 128:(c + 1) * 128], rhs=hT[:, kk, :],
                                            start=(kk == 0), stop=(kk == 15))
                        return last
                    k.op("tensor", mm, reads=[wb, hT], writes=[p_])
                    flush()
                    if chunk < 13:
                        if chunk < 6:
                            name, r0 = "qbT", chunk * 128
                        elif chunk < 12:
                            name, r0 = "kbT", (chunk - 6) * 128
                        else:
                            name, r0 = "kkT", 0
                        pend.append(rope_store(k, G, C, p_, pr, cs, SC[name][r0:r0 + 128, tok0:tok0 + 512],
                                               k.dbuf((name, seq, r0 // 128, g)), rcnt))
                        rcnt += 1
                    else:
                        uc = chunk - 13
                        o = uo[uc % 2]
                        k.op("scalar", lambda e: e.copy(out=o[:, :], in_=p_[:, :]), reads=[p_], writes=[o])
                        k.dma("sync", SC["ucT"][uc * 128:(uc + 1) * 128, tok0:tok0 + 512], o[:, :], o,
                              reads=[o], writes=[k.dbuf(("ucT", seq, uc, g))])
                    yield 4.3
                idx += 1
            if idx + 1 < total:
                issue(idx + 1)
            wb = wbs[idx % 2]
            wv = wview(wb, 16, 464)
            for i in range(4):
                p_ = ps[cnt % 2]
                cnt += 1

                def mm(e):
                    last = None
                    for kk in range(16):
                        last = e.matmul(p_[:, 0:464], lhsT=hT[:, kk, i * 128:(i + 1) * 128], rhs=wv[:, kk, :],
                                        start=(kk == 0), stop=(kk == 15))
                    return last
                k.op("tensor", mm, reads=[wb, hT], writes=[p_])
                flush()
                s_, r_, cn = css[i % 2], crs[i % 2], cqn[i % 2]
                k.op("vector", lambda e: e.memset(s_[:, :], 0.0), writes=[s_])
                k.op("scalar", lambda e: e.activation(out=cjunk[:, :], in_=p_[:, 0:384], func=AF.Square,
                                                      accum_out=s_[:, 0:1]), reads=[p_], writes=[cjunk, s_])
                rstd_from_ss(k, s_, r_, 384)
                k.op("vector", lambda e: e.scalar_tensor_tensor(out=cn[:, :], in0=p_[:, 0:384], scalar=r_[:, 0:1],
                                                                in1=gcq[:, :], op0=ALU.mult, op1=ALU.mult),
                     reads=[p_, r_, gcq], writes=[cn])
                k.op("scalar", lambda e: e.copy(out=va_g[:, i, :], in_=p_[:, 384:448]), reads=[p_], writes=[va_g])
                k.op("scalar", lambda e: e.mul(out=wi_g[:, i, :], in_=p_[:, 448:464], mul=1.0 / 32.0), reads=[p_],
                     writes=[wi_g])
                ptv = pst.t[:].bitcast(BF16)

                def tr(e, cn=cn, ptv=ptv):
                    last = None
                    for j in range(3):
                        last = e.transpose(out=ptv[:, j * 128:(j + 1) * 128], in_=cn[:, j * 128:(j + 1) * 128],
                                           identity=ident[:, :])
                    return last
                def trp(tr=tr, cn=cn, i=i, ptv=ptv):
                    k.op("tensor", tr, reads=[cn, ident], writes=[pst])
                    k.op("vector", lambda e: e.tensor_copy(out=cqT[:, :, i * 128:(i + 1) * 128],
                                                           in_=ptv[:, 0:384].rearrange("p (j t) -> p j t", j=3)),
                         reads=[pst], writes=[cqT])
                pend.append(trp)
                yield 4.0
            idx += 1
            k.dma("sync", SC["va"][tok0:tok0 + 512, :].rearrange("(i p) c -> p i c", p=128), va_g[:, :, :],
                  va_g, reads=[va_g], writes=[k.dbuf(("va", seq, g))])
            k.dma("sync", SC["wi"][tok0:tok0 + 512, :].rearrange("(i p) c -> p i c", p=128), wi_g[:, :, :],
                  wi_g, reads=[wi_g], writes=[k.dbuf(("wi", seq, g))])
            for hv in range(2):
                if idx + 1 < total:
                    issue(idx + 1)
                wb = wbs[idx % 2]
                wv = wview(wb, 16, 384)
                for i in range(4):
                    p_ = ps[cnt % 2]
                    cnt += 1

                    def mm(e):
                        last = None
                        for kk in range(16):
                            last = e.matmul(p_[:, 0:384], lhsT=hT[:, kk, i * 128:(i + 1) * 128], rhs=wv[:, kk, :],
                                            start=(kk == 0), stop=(kk == 15))
                        return last
                    k.op("tensor", mm, reads=[wb, hT], writes=[p_])
                    flush()
                    if i % 2 == 0:
                        k.op("scalar", lambda e: e.copy(out=vb_g[:, i, hv * 384:(hv + 1) * 384], in_=p_[:, 0:384]),
                             reads=[p_], writes=[vb_g])
                    else:
                        k.op("vector", lambda e: e.tensor_copy(out=vb_g[:, i, hv * 384:(hv + 1) * 384],
                                                               in_=p_[:, 0:384]), reads=[p_], writes=[vb_g])
                    yield 3.0
                idx += 1
            k.dma("sync", SC["vb"][tok0:tok0 + 512, :].rearrange("(i p) c -> p i c", p=128), vb_g[:, :, :],
                  vb_g, reads=[vb_g], writes=[k.dbuf(("vb", seq, g))])
            flush()
            for c in range(12):
                p_ = ps[cnt % 2]
                cnt += 1

                def mm(e):
                    last = None
                    for kk in range(3):
                        last = e.matmul(p_[:, :], lhsT=wuq[:, kk, c * 128:(c + 1) * 128], rhs=cqT[:, kk, :],
                                        start=(kk == 0), stop=(kk == 2))
                    return last
                k.op("tensor", mm, reads=[wuq, cqT], writes=[p_])
                flush()
                if c < 4:
                    name, r0 = "qaT", c * 128
                else:
                    name, r0 = "qiT", (c - 4) * 128
                pend.append(rope_store(k, G, C, p_, pr, cs, SC[name][r0:r0 + 128, tok0:tok0 + 512],
                                       k.dbuf((name, seq, r0 // 128, g)), rcnt))
                rcnt += 1
                yield 2.0
            flush()


def phase_wo(k, G, l, seq, ngroups, ps, side):
    SC = G["sc"][seq]
    xres = G["xres"]
    with k.phase() as st:
        sb = lambda shape, dt, nm: k.sbuf(st, shape, dt, nm, side)
        mixg = [sb([128, 16, 512], BF16, "mixg") for _ in range(2)]
        wbs = [sb([128, 8192], BF16, "wb") for _ in range(3)]
        ot = [sb([128, 512], F32, "ot") for _ in range(4)]
        wo = G["w_o"][l]
        total = ngroups * 4

        def issue(idx):
            wslab_load(k, wbs[idx % 3], wo, (idx % 4) * 512, 512)
        issue(0)
        issue(1)
        idx = 0
        ocnt = 0
        pend = [None]
        for g in range(ngroups):
            mg = mixg[g % 2]
            rd = [k.dbuf(("mixTa", seq, n)) for n in range(4 * g, 4 * g + 4)]
            rd += [k.dbuf(("mixT", seq, rc)) for rc in range(4, 10)]
            rd += [k.dbuf(("mixTc", seq, gg, m, g)) for gg in range(4) for m in range(2)]
            k.dma("sync", mg[:, :, :], SC["mixT"].rearrange("(k p) t -> p k t", p=128)[:, :, g * 512:(g + 1) * 512],
                  mg, reads=rd, writes=[mg])
            for cg in range(4):
                if idx + 2 < total:
                    issue(idx + 2)
                wb = wbs[idx % 3]
                wv = wview(wb, 16, 512)
                for i in range(4):
                    p_ = ps[ocnt % len(ps)]
                    o = ot[ocnt % 4]
                    ocnt += 1

                    def mm(e):
                        last = None
                        for kk in range(16):
                            last = e.matmul(p_[:, :], lhsT=mg[:, kk, i * 128:(i + 1) * 128], rhs=wv[:, kk, :],
                                            start=(kk == 0), stop=(kk == 15))
                        return last
                    k.op("tensor", mm, reads=[wb, mg], writes=[p_])
                    if pend[0] is not None:
                        pend[0]()

                    def ev(p_=p_, o=o, i=i, cg=cg, g=g):
                        if i % 2 == 0:
                            k.op("scalar", lambda e: e.copy(out=o[:, :], in_=p_[:, :]), reads=[p_], writes=[o])
                        else:
                            k.op("vector", lambda e: e.tensor_copy(out=o[:, :], in_=p_[:, :]), reads=[p_], writes=[o])
                        tile = g * 4 + i
                        k.dma("gpsimd", xres[seq, tile * 128:(tile + 1) * 128, cg * 512:(cg + 1) * 512], o[:, :], o,
                              reads=[o], writes=xbufs(k, seq, tile, cg * 2, cg * 2 + 2), accum_op=ALU.add)
                    pend[0] = ev
                    yield 4.0
                idx += 1
        if pend[0] is not None:
            pend[0]()


def sl(start, n, d):
    return slice(start, start + (n - 1) * d + 1, d)


def band_mask():
    kk = np.arange(128)[:, None]
    qq = np.arange(256)[None, :]
    return ((qq >= kk) & (qq <= kk + 128)).astype(np.float32)


def phase_mixB(k, G, seq, S_, ps, side):
    SC = G["sc"][seq]
    nblk = S_ // 128
    HQ = S_ // 2
    pats = [(1, 0), (4, 1), (16, 2)]
    with k.phase() as st:
        sb = lambda shape, dt, nm: k.sbuf(st, shape, dt, nm, side)
        vb = SC["vb"]
        vreads = [k.dbuf(("vb", seq, g)) for g in range(S_ // 512)]
        vaug = [[sb([128, nblk, 128], BF16, "vaug") for _ in range(3)] for _ in range(2)]
        for a in range(2):
            for o in range(3):
                k.op("vector", lambda e: e.memset(vaug[a][o][:, :, 64:128], 1.0), writes=[vaug[a][o]])
        qT = [sb([64, S_], BF16, "qT") for _ in range(2)]
        kT = [sb([64, S_], BF16, "kT") for _ in range(2)]
        pt = [sb([128, 256], BF16, "pt") for _ in range(8)]
        rc = [sb([64, 512], F32, "rc") for _ in range(2)]
        mixc = [sb([128, S_], BF16, "mixc") for _ in range(2)]
        band = G["band"]
        scnt = 0
        for h in range(12):
            a = h % 2
            q_, k_ = qT[a], kT[a]
            qreads = [k.dbuf(("qbT", seq, h // 2, g)) for g in range(S_ // 512)]
            kreads = [k.dbuf(("kbT", seq, h // 2, g)) for g in range(S_ // 512)]
            k.dma("sync", q_[:, :], SC["qbT"][h * 64:(h + 1) * 64, :], q_, reads=qreads, writes=[q_])
            k.dma("sync", k_[:, :], SC["kbT"][h * 64:(h + 1) * 64, :], k_, reads=kreads, writes=[k_])
            vh = vb[:, h * 64:(h + 1) * 64]
            k.dma("sync", vaug[a][0][:, :, 0:64], vh.rearrange("(j p) c -> p j c", p=128), vaug[a][0], reads=vreads,
                  writes=[vaug[a][0]])
            for r in range(4):
                k.dma("sync", vaug[a][1][:, r * (nblk // 4):(r + 1) * (nblk // 4), 0:64],
                      vh.rearrange("(j p r) c -> p r j c", p=128, r=4)[:, r, :, :], vaug[a][1], reads=vreads,
                      writes=[vaug[a][1]])
            k.dma("sync", vaug[a][2][:, :, 0:64], vh.rearrange("(p r) c -> p r c", r=16), vaug[a][2], reads=vreads,
                  writes=[vaug[a][2]])
            mc = mixc[(h // 2) % 2]
            for hf in range(2):
                for b in range(2):
                    k.op("vector", lambda e: e.memset(ps[b][:, :], 0.0), writes=[ps[b]])
                units = []
                for (d, o) in pats:
                    ld = S_ // d
                    nb = ld // 128
                    P2 = HQ // d
                    for r in range(d):
                        for j in range(nb):
                            qa_ = max(j * 128, hf * P2)
                            qb_ = min(j * 128 + 256, ld, (hf + 1) * P2)
                            if qa_ < qb_:
                                units.append((d, o, r, j, nb, qa_ - j * 128, qb_ - qa_))
                LA = 6
                ptl = {}
                for u in range(len(units) + LA):
                    if u < len(units):
                        d, o, r, j, nb, qlo, nq = units[u]
                        kt0 = j * 128 * d + r
                        qt0 = (j * 128 + qlo) * d + r
                        p_s = ps[2 + scnt % 3]
                        p_ = pt[scnt % 8]
                        scnt += 1
                        ptl[u] = p_
                        k.op("tensor", lambda e: e.matmul(p_s[:, 0:nq], lhsT=k_[:, sl(kt0, 128, d)],
                                                          rhs=q_[:, sl(qt0, nq, d)], start=True, stop=True),
                             reads=[k_, q_], writes=[p_s])
                        k.op("scalar", lambda e: e.activation(out=p_[:, 0:nq], in_=p_s[:, 0:nq], func=AF.Exp,
                                                              scale=0.125), reads=[p_s], writes=[p_])
                        k.op("vector", lambda e: e.tensor_tensor(out=p_[:, 0:nq], in0=p_[:, 0:nq],
                                                                 in1=band[:, qlo:qlo + nq], op=ALU.mult),
                             reads=[p_, band], writes=[p_])
                    if u - LA >= 0:
                        d, o, r, j, nb, qlo, nq = units[u - LA]
                        p_ = ptl.pop(u - LA)
                        qt0 = (j * 128 + qlo) * d + r - hf * HQ
                        blk = r * nb + j
                        va_ = vaug[a][o]
                        qi = 0
                        while qi < nq:
                            tok = qt0 + qi * d
                            bank = tok // 512
                            n = min(nq - qi, (512 * (bank + 1) - tok + d - 1) // d)
                            pb = ps[bank]
                            c0 = tok - bank * 512
                            k.op("tensor", lambda e: e.matmul(pb[:, sl(c0, n, d)], lhsT=va_[:, blk, :],
                                                              rhs=p_[:, qi:qi + n], start=False, stop=True,
                                                              skip_group_check=True),
                                 reads=[va_, p_], writes=[pb])
                            qi += n
                    yield 0.7
                for b in range(2):
                    pb = ps[b]
                    r_ = rc[b % 2]
                    k.op("scalar", lambda e: e.activation(out=r_[:, :], in_=pb[64:128, :], func=AF.Ln), reads=[pb],
                         writes=[r_])
                    k.op("scalar", lambda e: e.activation(out=r_[:, :], in_=r_[:, :], func=AF.Exp, scale=-1.0),
                         reads=[r_], writes=[r_])
                    c0 = hf * HQ + b * 512
                    k.op("vector", lambda e: e.tensor_tensor(out=mc[a * 64:(a + 1) * 64, c0:c0 + 512],
                                                             in0=pb[0:64, :], in1=r_[:, :], op=ALU.mult),
                         reads=[pb, r_], writes=[mc])
                yield 2.0
            if a == 1:
                row0 = 512 + (h // 2) * 128
                k.dma("sync", SC["mixT"][row0:row0 + 128, :], mc[:, :], mc, reads=[mc],
                      writes=[k.dbuf(("mixT", seq, row0 // 128))])


def phase_mixC(k, G, l, seq, S_, ps, side):
    SC = G["sc"][seq]
    PADC = 16
    with k.phase() as st:
        sb = lambda shape, dt, nm: k.sbuf(st, shape, dt, nm, side)
        U = sb([128, PADC + S_], F32, "U")
        A = sb([128, PADC + S_], F32, "A")
        B = sb([128, PADC + S_], F32, "B")
        Yb = [sb([128, S_], BF16, "Yb") for _ in range(2)]
        tmpc = sb([128, 16], F32, "tmpc")
        wp = [sb([128, 192], BF16, "wp0"), sb([64, 192], BF16, "wp1")]
        psc = sb([128, 8], F32, "psc")
        ob = [sb([128, 512], BF16, "obc") for _ in range(2)]
        invc = G["invc"]
        for t in (U, A, B):
            k.op("vector", lambda e: e.memset(t[:, 0:PADC], 0.0), writes=[t])
        for g in range(4):
            for m, (d0, n) in enumerate([(0, 128), (128, 64)]):
                k.dma("sync", psc[0:n, g * 2 + m:g * 2 + m + 1],
                      G["pool_scale"][l, g * 192 + d0:g * 192 + d0 + n].rearrange("(p o) -> p o", o=1), psc,
                      writes=[psc])
        ureads = lambda ch: [k.dbuf(("ucT", seq, ch, gg)) for gg in range(S_ // 512)]
        cnt = 0
        for g in range(4):
            w = 2 ** (g + 1)
            k.dma("gpsimd", wp[0][:, :], G["w_pool"][l, g, 0:128, :], wp[0], writes=[wp[0]])
            k.dma("gpsimd", wp[1][:, :], G["w_pool"][l, g, 128:192, :], wp[1], writes=[wp[1]])
            for m, (c0, n) in enumerate([(0, 128), (128, 64)]):
                r0 = g * 192 + c0
                chs = sorted(set([r0 // 128, (r0 + n - 1) // 128]))
                rd = []
                for ch in chs:
                    rd += ureads(ch)
                k.dma("sync", U[0:n, PADC:PADC + S_], SC["ucT"][r0:r0 + n, :], U, reads=rd, writes=[U])
                src = U
                bufs = [A, B]
                sh = 1
                bi = 0
                while sh < w:
                    dst = bufs[bi % 2]
                    k.op("vector", lambda e: e.tensor_tensor(out=dst[0:n, PADC:PADC + S_], in0=src[0:n, PADC:PADC + S_],
                                                             in1=src[0:n, PADC - sh:PADC - sh + S_], op=ALU.add),
                         reads=[src], writes=[dst])
                    src = dst
                    bi += 1
                    sh *= 2
                    yield 2.2
                Y = Yb[m]
                k.op("vector", lambda e: e.scalar_tensor_tensor(out=Y[0:n, :], in0=src[0:n, PADC:PADC + S_],
                                                                scalar=1.0 / w, in1=U[0:n, PADC:PADC + S_],
                                                                op0=ALU.mult, op1=ALU.subtract),
                     reads=[src, U], writes=[Y])
                k.op("vector", lambda e: e.tensor_tensor(out=tmpc[0:n, 0:w - 1], in0=src[0:n, PADC:PADC + w - 1],
                                                         in1=invc[0:n, 0:w - 1], op=ALU.mult),
                     reads=[src, invc], writes=[tmpc])
                k.op("vector", lambda e: e.tensor_tensor(out=Y[0:n, 0:w - 1], in0=tmpc[0:n, 0:w - 1],
                                                         in1=U[0:n, PADC:PADC + w - 1], op=ALU.subtract),
                     reads=[tmpc, U], writes=[Y])
                yield 2.5
            for m, (d0, n) in enumerate([(0, 128), (128, 64)]):
                for tc in range(S_ // 512):
                    p_ = ps[cnt % len(ps)]
                    o = ob[cnt % 2]
                    cnt += 1

                    def mm(e):
                        e.matmul(p_[0:n, :], lhsT=wp[0][:, d0:d0 + n], rhs=Yb[0][:, tc * 512:(tc + 1) * 512],
                                 start=True, stop=False)
                        return e.matmul(p_[0:n, :], lhsT=wp[1][0:64, d0:d0 + n],
                                        rhs=Yb[1][0:64, tc * 512:(tc + 1) * 512], start=False, stop=True)
                    k.op("tensor", mm, reads=[wp[0], wp[1], Yb[0], Yb[1]], writes=[p_])
                    k.op("scalar", lambda e: e.activation(out=o[0:n, :], in_=p_[0:n, :], func=AF.Copy,
                                                          scale=psc[0:n, g * 2 + m:g * 2 + m + 1]),
                         reads=[p_, psc], writes=[o])
                    row0 = 1280 + g * 192 + d0
                    k.dma("sync", SC["mixT"][row0:row0 + n, tc * 512:(tc + 1) * 512], o[0:n, :], o, reads=[o],
                          writes=[k.dbuf(("mixTc", seq, g, m, tc))])
                    yield 0.7


def caus_neg():
    t = np.arange(128)[:, None]
    s = np.arange(128)[None, :]
    return np.where(s <= t, 0.0, -1e30).astype(np.float32)


def phase_mixA(k, G, seq, S_, ps, side):
    SC = G["sc"][seq]
    nblk = S_ // 128
    ng = S_ // 512
    with k.phase() as st:
        sb = lambda shape, dt, nm: k.sbuf(st, shape, dt, nm, side)
        ki2 = sb([128, S_], BF16, "ki2")
        ka = sb([64, S_], BF16, "ka")
        va_aug = sb([128, nblk, 128], BF16, "va_aug")
        kkreads = [k.dbuf(("kkT", seq, 0, g)) for g in range(ng)]
        k.dma("sync", ki2[0:64, :], SC["kkT"][64:128, :], ki2, reads=kkreads, writes=[ki2])
        k.dma("sync", ki2[64:128, :], SC["kkT"][64:128, :], ki2, reads=kkreads, writes=[ki2])
        k.dma("sync", ka[:, :], SC["kkT"][0:64, :], ka, reads=kkreads, writes=[ka])
        k.op("vector", lambda e: e.memset(va_aug[:, :, 64:128], 1.0), writes=[va_aug])
        k.dma("sync", va_aug[:, :, 0:64], SC["va"].rearrange("(j p) c -> p j c", p=128), va_aug,
              reads=[k.dbuf(("va", seq, g)) for g in range(ng)], writes=[va_aug])
        acc = [sb([128, S_], F32, "acc") for _ in range(2)]
        rr = [sb([128, 512], F32, "rr") for _ in range(2)]
        msk = [sb([128, S_], BF16, "msk") for _ in range(2)]
        mskT = [sb([128, nblk, 128], BF16, "mskT") for _ in range(2)]
        qi_n = [sb([128, 8, 128], BF16, "qi_n") for _ in range(2)]
        qa_n = [sb([64, 8, 128], BF16, "qa_n") for _ in range(3)]
        wi_n = [sb([128, 16], F32, "wi_n") for _ in range(2)]
        P = [sb([128, 512], BF16, "P") for _ in range(3)]
        oA = [sb([64, 1024], BF16, "oA") for _ in range(2)]
        rc = [sb([64, 512], F32, "rcA")]
        sc1 = lambda nm: [sb([128, 1], F32, nm) for _ in range(2)]
        mx, rng_, lo, cand, ge = sc1("mx"), sc1("rng"), sc1("lo"), sc1("cand"), sc1("ge")
        sgs = [sb([128, NI_BISECT + 1], F32, "sgs") for _ in range(2)]
        ident = G["ident"]
        causn = G["causn"]
        dc = [0]

        def gen_I(n, i):
            S = (n + 1) * 128
            g = n // 4
            qi, qa, wi = qi_n[i % 2], qa_n[i % 3], wi_n[i % 2]
            a1 = acc[i % 2]
            k.dma("sync", qi[:, :, :], SC["qiT"][:, n * 128:(n + 1) * 128].rearrange("(c p) t -> p c t", p=128),
                  qi, reads=[k.dbuf(("qiT", seq, c, g)) for c in range(8)], writes=[qi])
            k.dma("sync", qa[:, :, :], SC["qaT"][:, n * 128:(n + 1) * 128].rearrange("(h d) t -> d h t", d=64),
                  qa, reads=[k.dbuf(("qaT", seq, c, g)) for c in range(4)], writes=[qa])
            k.dma("sync", wi[:, :], SC["wi"][n * 128:(n + 1) * 128, :], wi, reads=[k.dbuf(("wi", seq, g))],
                  writes=[wi])
            k.op("vector", lambda e: e.memset(a1[:, 0:S], 0.0), writes=[a1])
            nkc = (S + 511) // 512
            for kc in range(nkc):
                wk = min(512, S - kc * 512)
                for head in range(16):
                    c, hh = divmod(head, 2)
                    p_ = ps[dc[0] % 2]
                    r_ = rr[dc[0] % 2]
                    dc[0] += 1
                    k.op("tensor", lambda e: e.matmul(p_[:, 0:wk], lhsT=qi[hh * 64:(hh + 1) * 64, c, :],
                                                      rhs=ki2[hh * 64:(hh + 1) * 64, kc * 512:kc * 512 + wk],
                                                      start=True, stop=True), reads=[qi, ki2], writes=[p_])
                    k.op("scalar", lambda e: e.activation(out=r_[:, 0:wk], in_=p_[:, 0:wk], func=AF.Relu),
                         reads=[p_], writes=[r_])
                    k.op("vector", lambda e: e.scalar_tensor_tensor(out=a1[:, kc * 512:kc * 512 + wk],
                                                                    in0=r_[:, 0:wk], scalar=wi[:, head:head + 1],
                                                                    in1=a1[:, kc * 512:kc * 512 + wk],
                                                                    op0=ALU.mult, op1=ALU.add),
                         reads=[r_, wi, a1], writes=[a1])
                    yield 0.6 * wk / 512
            if S > TOPK:
                l_, m_, r2 = lo[i % 2], mx[i % 2], rng_[i % 2]
                k.op("vector", lambda e: e.tensor_reduce(out=m_[:, 0:1], in_=a1[:, 0:S], axis=AX.X, op=ALU.max),
                     reads=[a1], writes=[m_])
                k.op("vector", lambda e: e.tensor_reduce(out=l_[:, 0:1], in_=a1[:, 0:S], axis=AX.X, op=ALU.min),
                     reads=[a1], writes=[l_])
                k.op("vector", lambda e: e.tensor_tensor(out=r2[:, 0:1], in0=m_[:, 0:1], in1=l_[:, 0:1],
                                                         op=ALU.subtract), reads=[m_, l_], writes=[r2])
            k.op("vector", lambda e: e.tensor_tensor(out=a1[:, n * 128:(n + 1) * 128], in0=a1[:, n * 128:(n + 1) * 128],
                                                     in1=causn[:, :], op=ALU.add), reads=[a1, causn], writes=[a1])
            yield 3.0

        def gen_T(n, i):
            S = (n + 1) * 128
            a1 = acc[i % 2]
            m_ = msk[i % 2]
            mT = mskT[i % 2]
            if S > TOPK:
                l_, r2, c_, g_, sg = lo[i % 2], rng_[i % 2], cand[i % 2], ge[i % 2], sgs[i % 2]
                k.op("vector", lambda e: e.memset(sg[:, :], 0.0), writes=[sg])
                for it in range(1, NI_BISECT + 1):
                    f = 2.0 ** (-it)
                    k.op("vector", lambda e: e.scalar_tensor_tensor(out=c_[:, 0:1], in0=r2[:, 0:1], scalar=-f,
                                                                    in1=l_[:, 0:1], op0=ALU.mult, op1=ALU.subtract),
                         reads=[r2, l_], writes=[c_])
                    k.op("scalar", lambda e: e.activation(out=m_[:, 0:S], in_=a1[:, 0:S], func=AF.Sign,
                                                          bias=c_[:, 0:1], scale=1.0, accum_out=sg[:, it:it + 1]),
                         reads=[a1, c_], writes=[m_, sg])
                    k.op("vector", lambda e: e.tensor_scalar(out=g_[:, 0:1], in0=sg[:, it:it + 1],
                                                             scalar1=float(2 * TOPK - 1 - S), scalar2=f, op0=ALU.is_gt,
                                                             op1=ALU.mult), reads=[sg], writes=[g_])
                    k.op("vector", lambda e: e.scalar_tensor_tensor(out=l_[:, 0:1], in0=g_[:, 0:1],
                                                                    scalar=r2[:, 0:1], in1=l_[:, 0:1],
                                                                    op0=ALU.mult, op1=ALU.add),
                         reads=[g_, r2, l_], writes=[l_])
                    yield 0.3 + 0.8 * S / 1024
                k.op("vector", lambda e: e.tensor_scalar(out=m_[:, 0:S], in0=a1[:, 0:S], scalar1=l_[:, 0:1],
                                                         scalar2=None, op0=ALU.is_gt), reads=[a1, l_], writes=[m_])
            else:
                k.op("vector", lambda e: e.tensor_scalar(out=m_[:, 0:S], in0=a1[:, 0:S], scalar1=-1e29,
                                                         scalar2=None, op0=ALU.is_gt), reads=[a1], writes=[m_])
            yield 1.0
            pt_ = ps[1]
            ptv = pt_.t[:].bitcast(BF16)
            for b0 in range(0, n + 1, 8):
                nb_ = min(8, n + 1 - b0)

                def tr(e):
                    last = None
                    for j in range(nb_):
                        last = e.transpose(out=ptv[:, j * 128:(j + 1) * 128], in_=m_[:, (b0 + j) * 128:(b0 + j + 1) * 128],
                                           identity=ident[:, :])
                    return last
                k.op("tensor", tr, reads=[m_, ident], writes=[pt_])
                k.op("scalar", lambda e: e.copy(out=mT[:, b0:b0 + nb_, :],
                                                in_=ptv[:, 0:nb_ * 128].rearrange("p (j t) -> p j t", j=nb_)),
                     reads=[pt_], writes=[mT])
                yield 1.0

        def gen_X(n, i):
            qa = qa_n[i % 3]
            mT = mskT[i % 2]
            pa = ps[4]
            o_ = oA[i % 2]
            pc = 0
            for hc in range(2):
                def qk(kb, pc_):
                    pS = ps[2 + pc_ % 2]
                    P_ = P[pc_ % 3]
                    k.op("tensor", lambda e: e.matmul(pS[:, :], lhsT=ka[:, kb * 128:(kb + 1) * 128],
                                                      rhs=qa[:, hc * 4:(hc + 1) * 4, :].rearrange("p h t -> p (h t)"),
                                                      start=True, stop=True), reads=[ka, qa], writes=[pS])
                    k.op("scalar", lambda e: e.activation(out=P_[:, :], in_=pS[:, :], func=AF.Exp, scale=0.125),
                         reads=[pS], writes=[P_])
                    k.op("vector", lambda e: e.tensor_tensor(
                        out=P_[:, :].rearrange("p (h t) -> p h t", h=4), in0=P_[:, :].rearrange("p (h t) -> p h t", h=4),
                        in1=mT[:, kb:kb + 1, :].broadcast_to([128, 4, 128]), op=ALU.mult),
                        reads=[P_, mT], writes=[P_])
                    return P_
                pend = qk(0, pc)
                pc += 1
                for kb in range(n + 1):
                    nxt = None
                    if kb + 1 <= n:
                        nxt = qk(kb + 1, pc)
                        pc += 1
                    P_ = pend
                    k.op("tensor", lambda e: e.matmul(pa[:, :], lhsT=va_aug[:, kb, :], rhs=P_[:, :], start=(kb == 0),
                                                      stop=(kb == n)), reads=[va_aug, P_], writes=[pa])
                    pend = nxt
                    yield 0.8
                r_ = rc[0]
                k.op("scalar", lambda e: e.activation(out=r_[:, :], in_=pa[64:128, :], func=AF.Ln), reads=[pa],
                     writes=[r_])
                k.op("scalar", lambda e: e.activation(out=r_[:, :], in_=r_[:, :], func=AF.Exp, scale=-1.0), reads=[r_],
                     writes=[r_])
                k.op("vector", lambda e: e.tensor_tensor(out=o_[:, hc * 512:(hc + 1) * 512], in0=pa[0:64, :],
                                                         in1=r_[:, :], op=ALU.mult), reads=[pa, r_], writes=[o_])
                yield 1.0
            k.dma("sync", SC["mixT"][0:512, n * 128:(n + 1) * 128].rearrange("(h d) t -> d h t", d=64),
                  o_[:, :].rearrange("p (h t) -> p h t", h=8), o_, reads=[o_], writes=[k.dbuf(("mixTa", seq, n))])

        blist = list(range(nblk))
        N = len(blist)
        order = sorted(range(N), key=lambda a: -blist[a])
        for step in range(N + 2):
            gens = []
            if step < N:
                gens.append(gen_I(blist[order[step]], step))
            if 0 <= step - 1 < N:
                gens.append(gen_T(blist[order[step - 1]], step - 1))
            if 0 <= step - 2 < N:
                gens.append(gen_X(blist[order[step - 2]], step - 2))
            vt = [0.0] * len(gens)
            alive = list(range(len(gens)))
            while alive:
                gi = min(alive, key=lambda a_: vt[a_])
                try:
                    c = next(gens[gi])
                    vt[gi] += c
                    yield c
                except StopIteration:
                    alive.remove(gi)


PARAMS = [("g_mix", [D]), ("w_in", [D, NCOL]), ("g_cq", [384]), ("w_uq", [384, 512]), ("w_uq_idx", [384, 1024]),
          ("w_pool", [4, 192, 192]), ("pool_scale", [768]), ("w_o", [D, D]), ("g_mlp", [D]), ("w_up", [D, DFF]),
          ("w_down", [DFF, D])]


def alloc_scratch(k, nseq, S_):
    sc = []
    for s in range(nseq):
        d = {}
        for name, shape, dt in [("qbT", [768, S_], BF16), ("kbT", [768, S_], BF16), ("kkT", [128, S_], BF16),
                                ("ucT", [768, S_], F32), ("qaT", [512, S_], BF16), ("qiT", [1024, S_], BF16),
                                ("va", [S_, 64], BF16), ("vb", [S_, 768], BF16), ("wi", [S_, 16], F32),
                                ("mixT", [2048, S_], BF16)]:
            d[name] = k.dram(f"sc_{name}_{s}", shape, dt)
        sc.append(d)
    return sc


def rope_tables(S_):
    half = 32
    inv = (10000.0 ** (-np.arange(half, dtype=np.float32) / half)).astype(np.float32)
    pos = np.arange(S_, dtype=np.float32)
    ang = pos[None, :] * inv[:, None]
    cosT = np.tile(np.cos(ang).astype(np.float32), (4, 1))
    sinT = np.tile(np.sin(ang).astype(np.float32), (4, 1))
    P = np.zeros((128, 128), np.float32)
    for m in range(128):
        if (m % 64) < 32:
            P[m + 32, m] = -1.0
        else:
            P[m - 32, m] = 1.0
    return cosT, sinT, P


def const_inputs(S_):
    cosT, sinT, P = rope_tables(S_)
    return {"ident": np.eye(128, dtype=np.float32), "cosT": cosT, "sinT": sinT, "ropeP": P, "band": band_mask(),
            "causn": caus_neg(), "invc": np.tile((1.0 / np.arange(1, 17, dtype=np.float32))[None, :], (128, 1))}


def build_full(nlayers, nseq, S_, overlap=True):
    assert nseq == 2
    k = K()
    nc = k.nc
    G = {}
    x = k.dram("x", [nseq, S_, D], F32, kind="ExternalInput")
    for name, shape in PARAMS:
        G[name] = k.dram(name, [nlayers] + shape, F32, kind="ExternalInput")
    G["g_final"] = k.dram("g_final", [D], F32, kind="ExternalInput")
    cd = {n: k.dram(n, s, F32, kind="ExternalInput") for n, s in
          [("ident", [128, 128]), ("cosT", [128, S_]), ("sinT", [128, S_]), ("ropeP", [128, 128]),
           ("band", [128, 256]), ("causn", [128, 128]), ("invc", [128, 16])]}
    G["cosT_d"], G["sinT_d"] = cd["cosT"], cd["sinT"]
    G["y"] = k.dram("y", [nseq, S_, D], F32, kind="ExternalOutput")
    G["xres"] = k.dram("xres", [nseq, S_, D], F32)
    G["sc"] = alloc_scratch(k, nseq, S_)
    ng = S_ // 512
    with k.es:
        k.ps = [T(k.es.enter_context(nc.psum_tensor(f"ps{i}", [128, 512], F32)), f"ps{i}", k.es) for i in range(8)]
        for nm, shape, dt, q in [("ident", [128, 128], BF16, "gpsimd"), ("ropeP", [128, 128], F32, "sync"),
                                 ("band", [128, 256], BF16, "gpsimd"), ("causn", [128, 128], F32, "sync"),
                                 ("invc", [128, 16], F32, "sync")]:
            t = k.sbuf(k.es, shape, dt, nm, "left")
            k.dma(q, t[:], cd[nm][:, :], t, writes=[t])
            G[nm] = t
        cs = DmaSem(k, "cpy")
        for s in range(nseq):
            for i in range(S_ // 128):
                k.dma("sync", G["xres"][s, i * 128:(i + 1) * 128, :], x[s, i * 128:(i + 1) * 128, :], cs,
                      writes=xbufs(k, s, i))
        HP, LP = k.ps[0:3], k.ps[3:8]

        def light(s, l):
            return chain(scaled(phase_mixB(k, G, s, S_, LP, "right"), 1.1),
                         scaled(phase_mixA(k, G, s, S_, LP, "right"), 1.9),
                         scaled(phase_mixC(k, G, l, s, S_, LP, "right"), 2.0))

        def h_in(s, l):
            return scaled(phase_in(k, G, l, s, ng, HP, "left"), 1.3)

        def h_out(s, l):
            return chain(phase_wo(k, G, l, s, ng, HP, "left"), phase_ffn(k, G, l, s, ng, HP, "left"))

        if overlap:
            drain(h_in(0, 0))
            for l in range(nlayers):
                hv = ([h_out(1, l - 1)] if l > 0 else []) + [h_in(1, l)]
                merge([chain(*hv), light(0, l)])
                hv = [h_out(0, l)] + ([h_in(0, l + 1)] if l + 1 < nlayers else [])
                merge([chain(*hv), light(1, l)])
            drain(h_out(1, nlayers - 1))
        else:
            for l in range(nlayers):
                for s in range(nseq):
                    drain(chain(h_in(s, l), light(s, l), h_out(s, l)))
        for s in range(nseq):
            drain(phase_final(k, G, s, S_ // 128, "left"))
    return k


N_CORES = 8
_PROG = {}


def kernel(x, g_mix, w_in, g_cq, w_uq, w_uq_idx, w_pool, pool_scale, w_o, g_mlp, w_up, w_down, g_final):
    x = np.asarray(x, dtype=np.float32)
    B, S_, _ = x.shape
    nlayers = int(np.asarray(g_mix).shape[0])
    nseq = B // N_CORES
    key = (nlayers, nseq, S_)
    if key not in _PROG:
        _PROG[key] = build_full(nlayers, nseq, S_)
    k = _PROG[key]
    f = lambda a: np.ascontiguousarray(np.asarray(a, dtype=np.float32))
    shared = {"g_mix": f(g_mix), "w_in": np.ascontiguousarray(np.asarray(w_in, dtype=np.float32)[:, :, w_in_perm()]),
              "g_cq": f(g_cq), "w_uq": f(w_uq), "w_uq_idx": f(w_uq_idx), "w_pool": f(w_pool),
              "pool_scale": f(pool_scale), "w_o": f(w_o), "g_mlp": f(g_mlp), "w_up": f(w_up), "w_down": f(w_down),
              "g_final": f(g_final)}
    shared.update(const_inputs(S_))
    in_maps = []
    for c in range(N_CORES):
        m = dict(shared)
        m["x"] = np.ascontiguousarray(x[c * nseq:(c + 1) * nseq])
        in_maps.append(m)
    res = run_bass_kernel_spmd(k.nc, in_maps, core_ids=list(range(N_CORES)))
    return np.concatenate([np.asarray(r["y"], dtype=np.float32) for r in res.results], axis=0)
```
